# Optimizing a Trainium2 kernel written in Bass

```python
import math
import jax, jax.numpy as jnp
from jax import lax
import numpy as np

D_MODEL = 1024
BATCH = 32
SEQ = 256
DEPTH = 2
DEC_BATCH = 8
DEC_SEQ = 1024
PAST_LEN = 256

GRID_W = 64
HEAD_DIM = 64
BLOCK = 128
ROPE_BASE = 10000.0
EPS = 1e-6
WIN_HEADS = 8
WIN_KV_HEADS = 2
WIN_GROUP = WIN_HEADS // WIN_KV_HEADS
WINDOW = 128
WIN_Q_W = WIN_HEADS * HEAD_DIM
WIN_KV_W = WIN_KV_HEADS * HEAD_DIM
ATTN_SCALE = HEAD_DIM ** -0.5
RET_HEADS = 4
RET_CHUNK = 128
RET_W = RET_HEADS * HEAD_DIM
RET_K_SCALE = HEAD_DIM ** -0.5
MLA_HEADS = 4
MLA_NOPE = 64
MLA_ROPE = 32
MLA_V = 64
MLA_KV_RANK = 128
MLA_QK = MLA_NOPE + MLA_ROPE
MLA_Q_W = MLA_HEADS * MLA_QK
MLA_SCALE = MLA_QK ** -0.5
D_IN = WIN_Q_W + 2 * WIN_KV_W + 4 * RET_W + MLA_Q_W + MLA_KV_RANK + MLA_ROPE
MIX_W = WIN_Q_W + RET_W + MLA_HEADS * MLA_V
D_FF = 4 * D_MODEL
N_MOD = 6

kernel_name = 'hybrid_prefix_diffusion_step'


def rms_norm(x, gain):
    xf = x.astype(jnp.float32)
    y = xf * lax.rsqrt(jnp.mean(xf * xf, axis=-1, keepdims=True) + EPS)
    return (y * gain.astype(jnp.float32)).astype(x.dtype)


def head_layer_norm(x, gain):
    mu = jnp.mean(x, axis=-1, keepdims=True)
    var = jnp.mean(jnp.square(x - mu), axis=-1, keepdims=True)
    y = ((x - mu) * lax.rsqrt(var + EPS)).reshape(x.shape[:-2] + (-1,))
    return y * gain.astype(jnp.float32)


def axial_angles(n_tokens, dim):
    rows = n_tokens // GRID_W
    t = jnp.arange(rows * GRID_W)
    row = (t // GRID_W).astype(jnp.float32)
    col = (t % GRID_W).astype(jnp.float32)
    n_freq = dim // 4
    inv_freq = ROPE_BASE ** (-jnp.arange(n_freq, dtype=jnp.float32) / n_freq)
    return row[:, None] * inv_freq, col[:, None] * inv_freq


def rotate(x, ang):
    x1, x2 = jnp.split(x, 2, axis=-1)
    c, s = jnp.cos(ang), jnp.sin(ang)
    return jnp.concatenate([x1 * c - x2 * s, x1 * s + x2 * c], axis=-1)


def axial_rope(x, ang_row, ang_col):
    xr, xc = jnp.split(x.astype(jnp.float32), 2, axis=-1)
    out = jnp.concatenate([rotate(xr, ang_row[:, None, :]), rotate(xc, ang_col[:, None, :])], axis=-1)
    return out.astype(x.dtype)


def sink_softmax(s, sink):
    m = jnp.maximum(jnp.max(s, axis=-1, keepdims=True), sink)
    p = jnp.exp(s - m)
    return p / (jnp.sum(p, axis=-1, keepdims=True) + jnp.exp(sink - m))


def window_attn_context(q, k, v, sink):
    b, l = q.shape[:2]
    qg = q.reshape(b, l, WIN_KV_HEADS, WIN_GROUP, HEAD_DIM)
    s = jnp.einsum('bqkgd,bskd->bkgqs', qg, k).astype(jnp.float32) * ATTN_SCALE
    p = sink_softmax(s, sink.astype(jnp.float32).reshape(1, WIN_KV_HEADS, WIN_GROUP, 1, 1))
    o = jnp.einsum('bkgqs,bskd->bqkgd', p.astype(v.dtype), v)
    return o.reshape(b, l, WIN_Q_W)


def window_attn_latent(q, k, v, k_ctx, v_ctx, sink):
    b, t = q.shape[:2]
    nb = t // BLOCK
    qb = q.reshape(b, nb, BLOCK, WIN_KV_HEADS, WIN_GROUP, HEAD_DIM)

    def neighbours(a):
        ap = jnp.pad(a, ((0, 0), (BLOCK, BLOCK), (0, 0), (0, 0))).reshape(b, nb + 2, BLOCK, WIN_KV_HEADS, HEAD_DIM)
        return jnp.concatenate([ap[:, :-2], ap[:, 1:-1], ap[:, 2:]], axis=2)

    kn, vn = neighbours(k), neighbours(v)
    qpos = jnp.arange(nb)[:, None] * BLOCK + jnp.arange(BLOCK)[None, :]
    kpos = jnp.arange(nb)[:, None] * BLOCK - BLOCK + jnp.arange(3 * BLOCK)[None, :]
    kp = kpos[:, None, :]
    valid = (jnp.abs(kp - qpos[:, :, None]) <= WINDOW) & (kp >= 0) & (kp < t)
    s_loc = jnp.einsum('bnqkgd,bnskd->bnkgqs', qb, kn).astype(jnp.float32) * ATTN_SCALE
    s_loc = jnp.where(valid[None, :, None, None], s_loc, -jnp.inf)
    s_ctx = jnp.einsum('bnqkgd,bskd->bnkgqs', qb, k_ctx).astype(jnp.float32) * ATTN_SCALE
    p = sink_softmax(jnp.concatenate([s_loc, s_ctx], axis=-1),
                     sink.astype(jnp.float32).reshape(1, 1, WIN_KV_HEADS, WIN_GROUP, 1, 1))
    p_loc = p[..., :3 * BLOCK].astype(v.dtype)
    p_ctx = p[..., 3 * BLOCK:].astype(v.dtype)
    o = (jnp.einsum('bnkgqs,bnskd->bnqkgd', p_loc, vn)
         + jnp.einsum('bnkgqs,bskd->bnqkgd', p_ctx, v_ctx))
    return o.reshape(b, t, WIN_Q_W)


def retention_scan(q, k, v, log_gamma, s0):
    b, t, h, d = q.shape
    nc = t // RET_CHUNK
    idx = jnp.arange(RET_CHUNK, dtype=jnp.float32)
    rel = idx[:, None] - idx[None, :]
    lower = rel >= 0
    intra = jnp.where(lower[None], jnp.exp(jnp.where(lower, rel, 0.0)[None] * log_gamma[:, None, None]), 0.0)
    q_dec = jnp.exp((idx[:, None] + 1.0) * log_gamma[None, :])
    k_dec = jnp.exp((RET_CHUNK - 1.0 - idx[:, None]) * log_gamma[None, :])
    c_dec = jnp.exp(RET_CHUNK * log_gamma)

    def chunks(a):
        return a.reshape(b, nc, RET_CHUNK, h, d).transpose(1, 0, 2, 3, 4)

    def step(state, inp):
        qc, kc, vc = inp
        scores = jnp.einsum('bihd,bjhd->bhij', qc, kc) * intra
        o = (jnp.einsum('bhij,bjhe->bihe', scores, vc)
             + jnp.einsum('bihd,bhde->bihe', qc * q_dec[:, :, None], state))
        state = c_dec[:, None, None] * state + jnp.einsum('bjhd,bjhe->bhde', kc * k_dec[:, :, None], vc)
        return state, o

    s_fin, o = lax.scan(step, s0, (chunks(q), chunks(k), chunks(v)))
    return o.transpose(1, 0, 2, 3, 4).reshape(b, t, h, d), s_fin


def bidir_retention(q, k, v, g, decay_logit, gn_gain, s0):
    dt = q.dtype
    qf = q.astype(jnp.float32)
    kf = k.astype(jnp.float32) * RET_K_SCALE
    vf = v.astype(jnp.float32)
    log_gamma = jax.nn.log_sigmoid(decay_logit.astype(jnp.float32))
    s0 = s0.astype(jnp.float32)
    o_f, s_f = retention_scan(qf, kf, vf, log_gamma[0], s0[:, 0])
    o_b, s_b = retention_scan(jnp.flip(qf, 1), jnp.flip(kf, 1), jnp.flip(vf, 1), log_gamma[1], s0[:, 1])
    o = o_f + jnp.flip(o_b, 1)
    y = head_layer_norm(o, gn_gain) * jax.nn.silu(g.astype(jnp.float32))
    return y.astype(dt), jnp.stack([s_f, s_b], axis=1)


def mla_expand(ckv_n, w_kv_b):
    kv = (ckv_n @ w_kv_b).reshape(ckv_n.shape[:2] + (MLA_HEADS, MLA_NOPE + MLA_V))
    return kv[..., :MLA_NOPE], kv[..., MLA_NOPE:]


def mla_attend(q_nope, q_rope, k_nope, k_rope, v):
    s = (jnp.einsum('bqhd,bshd->bhqs', q_nope, k_nope)
         + jnp.einsum('bqhr,bsr->bhqs', q_rope, k_rope)).astype(jnp.float32) * MLA_SCALE
    p = jax.nn.softmax(s, axis=-1).astype(v.dtype)
    return jnp.einsum('bhqs,bshd->bqhd', p, v)


def mla_latent(q_nope, q_rope, k_nope, k_rope, v):
    b, t = q_nope.shape[:2]
    nb = t // BLOCK

    def blocks(a):
        return a.reshape((b, nb, BLOCK) + a.shape[2:]).swapaxes(0, 1)

    o = lax.map(lambda qs: mla_attend(qs[0], qs[1], k_nope, k_rope, v), (blocks(q_nope), blocks(q_rope)))
    return o.swapaxes(0, 1).reshape(b, t, MLA_HEADS * MLA_V)


def in_split_points():
    sizes = (WIN_Q_W, WIN_KV_W, WIN_KV_W, RET_W, RET_W, RET_W, RET_W, MLA_Q_W, MLA_KV_RANK, MLA_ROPE)
    points, acc = [], 0
    for s in sizes[:-1]:
        acc += s
        points.append(acc)
    return points


def trunk_layer(x, mod, lp, ctx=None):
    norm1, norm2, w_in, sink, decay, gn, kvn, w_kv_b, w_out, w_up, w_down = lp
    b, n, _ = x.shape
    shift1, scale1, gate1, shift2, scale2, gate2 = [mod[:, i][:, None, :] for i in range(N_MOD)]
    h = rms_norm(x, norm1) * (1.0 + scale1) + shift1
    proj = h @ w_in
    qa, ka, va, qb, kb, vb, gb, qc, ckv, krope = jnp.split(proj, in_split_points(), axis=-1)
    qa = qa.reshape(b, n, WIN_HEADS, HEAD_DIM)
    ka = ka.reshape(b, n, WIN_KV_HEADS, HEAD_DIM)
    va = va.reshape(b, n, WIN_KV_HEADS, HEAD_DIM)
    qb = qb.reshape(b, n, RET_HEADS, HEAD_DIM)
    kb = kb.reshape(b, n, RET_HEADS, HEAD_DIM)
    vb = vb.reshape(b, n, RET_HEADS, HEAD_DIM)
    qc = qc.reshape(b, n, MLA_HEADS, MLA_QK)
    q_nope, q_rope = qc[..., :MLA_NOPE], qc[..., MLA_NOPE:]
    ckv_n = rms_norm(ckv, kvn)
    k_nope, v_c = mla_expand(ckv_n, w_kv_b)
    if ctx is None:
        out_a = window_attn_context(qa, ka, va, sink)
        s_zero = jnp.zeros((b, 2, RET_HEADS, HEAD_DIM, HEAD_DIM), jnp.float32)
        out_b, s_ret = bidir_retention(qb, kb, vb, gb, decay, gn, s_zero)
        out_c = mla_attend(q_nope, q_rope, k_nope, krope, v_c).reshape(b, n, MLA_HEADS * MLA_V)
        new = (ka, va, ckv_n, krope, s_ret.astype(x.dtype))
    else:
        c_k, c_v, c_ckv, c_krope, c_state = ctx
        ar, ac = axial_angles(n, HEAD_DIM)
        out_a = window_attn_latent(axial_rope(qa, ar, ac), axial_rope(ka, ar, ac), va, c_k, c_v, sink)
        out_b, _ = bidir_retention(qb, kb, vb, gb, decay, gn, c_state)
        mr, mc = axial_angles(n, MLA_ROPE)
        q_rope = axial_rope(q_rope, mr, mc)
        krope = axial_rope(krope[:, :, None, :], mr, mc)[:, :, 0, :]
        ck_nope, cv_c = mla_expand(c_ckv, w_kv_b)
        out_c = mla_latent(q_nope, q_rope,
                           jnp.concatenate([k_nope, ck_nope], axis=1),
                           jnp.concatenate([krope, c_krope], axis=1),
                           jnp.concatenate([v_c, cv_c], axis=1))
        new = None
    mix = jnp.concatenate([out_a, out_b, out_c], axis=-1) @ w_out
    x = x + gate1 * mix
    h2 = rms_norm(x, norm2) * (1.0 + scale2) + shift2
    x = x + gate2 * (jnp.square(jax.nn.relu(h2 @ w_up)) @ w_down)
    return x, new


def setup_inputs(seed: int = 0) -> dict:
    key = jax.random.key(seed)
    ks = jax.random.split(key, 26)

    def nrm(k, shape, s=1.0):
        return jax.random.normal(k, shape, jnp.float32) * s

    base = 1.0 - 2.0 ** (-5.0 - jnp.arange(RET_HEADS, dtype=jnp.float32))
    decay_logit = jnp.log(base) - jnp.log1p(-base)
    return {
        'x_prompt': nrm(ks[0], (BATCH, SEQ, D_MODEL)),
        'x_sample': nrm(ks[1], (DEC_BATCH, DEC_SEQ, D_MODEL)),
        'cache_win_k': nrm(ks[2], (DEC_BATCH, DEPTH, PAST_LEN, WIN_KV_HEADS, HEAD_DIM)),
        'cache_win_v': nrm(ks[3], (DEC_BATCH, DEPTH, PAST_LEN, WIN_KV_HEADS, HEAD_DIM)),
        'cache_mla_ckv': nrm(ks[4], (DEC_BATCH, DEPTH, PAST_LEN, MLA_KV_RANK)),
        'cache_mla_krope': nrm(ks[5], (DEC_BATCH, DEPTH, PAST_LEN, MLA_ROPE)),
        'state_ret': nrm(ks[6], (DEC_BATCH, DEPTH, 2, RET_HEADS, HEAD_DIM, HEAD_DIM), 0.5),
        'c': nrm(ks[7], (DEC_BATCH, D_MODEL)),
        'c_ctx': nrm(ks[8], (D_MODEL,)),
        'w_mod': nrm(ks[9], (DEPTH, D_MODEL, N_MOD * D_MODEL), 0.5 * D_MODEL ** -0.5),
        'b_mod': nrm(ks[10], (DEPTH, N_MOD * D_MODEL), 0.02),
        'norm1': 1.0 + nrm(ks[11], (DEPTH, D_MODEL), 0.02),
        'norm2': 1.0 + nrm(ks[12], (DEPTH, D_MODEL), 0.02),
        'w_in': nrm(ks[13], (DEPTH, D_MODEL, D_IN), D_MODEL ** -0.5),
        'win_sink': nrm(ks[14], (DEPTH, WIN_HEADS), 0.5),
        'ret_decay': decay_logit[None, None, :] + nrm(ks[15], (DEPTH, 2, RET_HEADS), 0.1),
        'ret_gn': 1.0 + nrm(ks[16], (DEPTH, RET_W), 0.02),
        'mla_kv_norm': 1.0 + nrm(ks[17], (DEPTH, MLA_KV_RANK), 0.02),
        'w_kv_b': nrm(ks[18], (DEPTH, MLA_KV_RANK, MLA_HEADS * (MLA_NOPE + MLA_V)), MLA_KV_RANK ** -0.5),
        'w_out': nrm(ks[19], (DEPTH, MIX_W, D_MODEL), MIX_W ** -0.5),
        'w_up': nrm(ks[20], (DEPTH, D_MODEL, D_FF), D_MODEL ** -0.5),
        'w_down': nrm(ks[21], (DEPTH, D_FF, D_MODEL), D_FF ** -0.5),
        'final_norm': 1.0 + nrm(ks[22], (D_MODEL,), 0.02),
    }


def reference(x_prompt, x_sample, cache_win_k, cache_win_v, cache_mla_ckv, cache_mla_krope, state_ret,
              c, c_ctx, w_mod, b_mod, norm1, norm2, w_in, win_sink, ret_decay, ret_gn, mla_kv_norm,
              w_kv_b, w_out, w_up, w_down, final_norm):
    silu_ctx = jax.nn.silu(c_ctx)[None, :]
    silu_lat = jax.nn.silu(c)
    xp, xs = x_prompt, x_sample
    new_k, new_v, new_ckv, new_kr, new_s = [], [], [], [], []
    for l in range(DEPTH):
        lp = (norm1[l], norm2[l], w_in[l], win_sink[l], ret_decay[l], ret_gn[l], mla_kv_norm[l],
              w_kv_b[l], w_out[l], w_up[l], w_down[l])
        mod_ctx = (silu_ctx @ w_mod[l] + b_mod[l]).reshape(1, N_MOD, D_MODEL)
        mod_lat = (silu_lat @ w_mod[l] + b_mod[l]).reshape(-1, N_MOD, D_MODEL)
        xp, (k_l, v_l, ckv_l, kr_l, s_l) = trunk_layer(xp, mod_ctx, lp)
        ctx_l = (cache_win_k[:, l], cache_win_v[:, l], cache_mla_ckv[:, l], cache_mla_krope[:, l], state_ret[:, l])
        xs, _ = trunk_layer(xs, mod_lat, lp, ctx_l)
        new_k.append(k_l)
        new_v.append(v_l)
        new_ckv.append(ckv_l)
        new_kr.append(kr_l)
        new_s.append(s_l)
    y_prompt = rms_norm(xp, final_norm)
    y_sample = rms_norm(xs, final_norm)
    return (y_prompt, y_sample, jnp.stack(new_k, axis=1), jnp.stack(new_v, axis=1),
            jnp.stack(new_ckv, axis=1), jnp.stack(new_kr, axis=1), jnp.stack(new_s, axis=1))
```

```python
import numpy as np
from contextlib import ExitStack
import concourse.bass as bass
import concourse.mybir as mybir
from concourse.bass_utils import run_bass_kernel_spmd

F32 = mybir.dt.float32
BF16 = mybir.dt.bfloat16
AF = mybir.ActivationFunctionType
ALU = mybir.AluOpType
AX = mybir.AxisListType

ENGS = ("pe", "act", "dve", "pool", "sp")
NDMASEM = 16
EPS = 1e-6
NCORES = 8
T = 1024
ATTN_SCALE = 64 ** -0.5
RET_K_SCALE = 64 ** -0.5
MLA_SCALE = 96 ** -0.5
NEG = -30000.0
NRING = 4


class Sched:
    def __init__(self, nc):
        self.nc = nc
        self.ops = {e: [] for e in ENGS}
        self.cnt = {e: 0 for e in ENGS}
        self.lastw = {}
        self.readers = {}
        self.seen = {e: {} for e in ENGS}
        self.dma_n = {e: 0 for e in ENGS}
        self.dma_tok = {e: [None] * NDMASEM for e in ENGS}
        self.last_pe = None

    def _need(self, eng, tok, waits):
        if tok is None:
            return
        sem, val, peng = tok
        if eng == "pe" and peng == "pe" and sem == "c_pe":
            return
        if self.seen[eng].get(sem, 0) >= val:
            return
        waits[sem] = max(waits.get(sem, 0), val)

    def add(self, eng, fn, reads=(), writes=(), dma=False, pe_rows=None):
        waits = {}
        if eng == "pe" and pe_rows is not None and self.last_pe is not None:
            (lo, hi), banks, ltok = self.last_pe
            if (pe_rows[1] <= lo or pe_rows[0] >= hi) and set(banks) & set(writes):
                sem, val, _ = ltok
                if self.seen[eng].get(sem, 0) < val:
                    waits[sem] = val
        for k in reads:
            self._need(eng, self.lastw.get(k), waits)
            if isinstance(k, tuple) and k[0] == "pb":
                for t in self.readers.get(k, ()):
                    if t[2] != eng:
                        self._need(eng, t, waits)
        for k in writes:
            self._need(eng, self.lastw.get(k), waits)
            for t in self.readers.get(k, ()):
                self._need(eng, t, waits)
        if dma:
            n = self.dma_n[eng]
            nd = 8 if eng == "pool" else NDMASEM
            slot = n % nd
            self._need(eng, self.dma_tok[eng][slot], waits)
            tok = ("dma_%s_%d" % (eng, slot), 16 * (n // nd + 1), eng)
            self.dma_tok[eng][slot] = tok
            self.dma_n[eng] = n + 1
        else:
            self.cnt[eng] += 1
            tok = ("c_" + eng, self.cnt[eng], eng)
        for s, v in waits.items():
            self.seen[eng][s] = max(self.seen[eng].get(s, 0), v)
        self.ops[eng].append((fn, list(waits.items()), tok, dma))
        for k in reads:
            self.readers.setdefault(k, []).append(tok)
        for k in writes:
            self.lastw[k] = tok
            self.readers[k] = []
        if eng == "pe":
            self.last_pe = (pe_rows if pe_rows is not None else (0, 128), list(writes), tok)
        return tok

    def sem_names(self):
        names = ["c_" + e for e in ENGS]
        for e in ENGS:
            names += ["dma_%s_%d" % (e, i) for i in range(min(NDMASEM, self.dma_n[e]))]
        return names

    def emit(self, block, sems):
        engobj = {"pe": "tensor", "act": "scalar", "dve": "vector", "pool": "gpsimd", "sp": "sync"}
        fin = {}
        for e in ENGS:
            if self.cnt[e] > 0:
                fin["c_" + e] = self.cnt[e]
            for t in self.dma_tok[e]:
                if t is not None:
                    fin[t[0]] = max(fin.get(t[0], 0), t[1])

        def make(e):
            def body(engine):
                for fn, waits, tok, dma in self.ops[e]:
                    for s, v in waits:
                        engine.wait_ge(sems[s], v)
                    fn(engine).then_inc(sems[tok[0]], 16 if dma else 1)
                if e == "sp":
                    for s, v in fin.items():
                        engine.wait_ge(sems[s], v)
            return body

        for e in ENGS:
            if self.ops[e] or e == "sp":
                getattr(block, engobj[e])(make(e))


def host_consts():
    j = np.arange(128, dtype=np.float32)[:, None]
    i = np.arange(128, dtype=np.float32)[None, :]
    cf = {}
    cf["ident"] = np.eye(128, dtype=np.float32)
    cf["A1"] = np.maximum(i - j, 0.0)
    cf["M1"] = (i >= j).astype(np.float32)
    cf["A2"] = np.maximum(j - i, 0.0)
    cf["M2"] = (j >= i).astype(np.float32)
    e1 = np.zeros((128, 128), np.float32)
    e1[:64] = i + 1.0
    e1[64:] = 128.0 - i
    cf["E1"] = e1
    cj = np.zeros((128, 128), np.float32)
    cj[:, 0] = 127.0 - j[:, 0]
    cj[:, 1] = j[:, 0]
    cf["cj"] = cj
    cfa = np.concatenate([cf[k] for k in ("ident", "A1", "M1", "A2", "M2", "E1", "cj")], axis=1)
    lo = np.where(j <= i, 0.0, NEG).astype(np.float32)
    hi = np.where(i <= j, 0.0, NEG).astype(np.float32)
    zo = np.zeros((128, 128), np.float32)
    zo[:, 64:] = 1.0
    cb = np.concatenate([np.eye(128, dtype=np.float32), np.ones((128, 128), np.float32),
                         np.tile(lo, (1, 4)), np.tile(hi, (1, 4)), zo], axis=1)
    t = np.arange(T)
    row = (t // 64).astype(np.float32)
    col = (t % 64).astype(np.float32)

    def tab(dim):
        nf = dim // 4
        inv = (10000.0 ** (-np.arange(nf, dtype=np.float32) / nf)).astype(np.float32)
        ar = row[None, :] * inv[:, None]
        ac = col[None, :] * inv[:, None]
        c = np.concatenate([np.cos(ar), np.cos(ar), np.cos(ac), np.cos(ac)], 0)
        s = np.concatenate([-np.sin(ar), np.sin(ar), -np.sin(ac), np.sin(ac)], 0)
        return c.astype(np.float32), s.astype(np.float32)

    ca, sa = tab(64)
    cm, sm = tab(32)
    rope = np.zeros((128, 4, T), np.float32)
    rope[:, 0] = np.tile(ca, (2, 1))
    rope[:, 1] = np.tile(sa, (2, 1))
    rope[64:96, 2] = cm
    rope[64:96, 3] = sm
    return cfa, cb, rope


def swap_pairs(a, w):
    n = a.shape[-1]
    return a.reshape(a.shape[:-1] + (n // (2 * w), 2, w))[..., ::-1, :].reshape(a.shape)


class KB:
    def __init__(self):
        self.nc = bass.Bass("TRN2", target_bir_lowering=False)
        self.S = Sched(self.nc)
        self.es = ExitStack()
        self.ring_i = 0
        self.ev_i = 0
        self.uid = 0

    def din(self, name, shape):
        return self.nc.dram_tensor(name, list(shape), F32, kind="ExternalInput").ap()

    def dout(self, name, shape):
        return self.nc.dram_tensor(name, list(shape), F32, kind="ExternalOutput").ap()

    def sb(self, name, shape, dt):
        return self.es.enter_context(self.nc.sbuf_tensor(name, list(shape), dt))

    def ps(self, name, shape, dt=F32):
        return self.es.enter_context(self.nc.psum_tensor(name, list(shape), dt))

    def mm(self, out, lhsT, rhs, start, stop, reads, writes):
        lo = lhsT.base_partition()
        rows = (lo - lo % 32, lo + ((lhsT.partition_size() + 31) // 32) * 32)
        self.S.add("pe", lambda e: e.matmul(out, lhsT, rhs, start=start, stop=stop), reads, writes, pe_rows=rows)

    def tr(self, out, in_, ident, reads, writes):
        self.S.add("pe", lambda e: e.transpose(out, in_, ident), reads, writes)

    def act(self, out, in_, func, reads, writes, **kw):
        self.S.add("act", lambda e: e.activation(out=out, in_=in_, func=func, **kw), reads, writes)

    def tt(self, out, in0, in1, op, reads, writes, eng="dve"):
        self.S.add(eng, lambda e: e.tensor_tensor(out=out, in0=in0, in1=in1, op=op), reads, writes)

    def ts(self, out, in0, s1, s2, op0, op1, reads, writes, eng="dve"):
        if s2 is None:
            self.S.add(eng, lambda e: e.tensor_scalar(out=out, in0=in0, scalar1=s1, scalar2=None, op0=op0), reads, writes)
        else:
            self.S.add(eng, lambda e: e.tensor_scalar(out=out, in0=in0, scalar1=s1, scalar2=s2, op0=op0, op1=op1), reads, writes)

    def stt(self, out, in0, scalar, in1, op0, op1, reads, writes, eng="dve"):
        self.S.add(eng, lambda e: e.scalar_tensor_tensor(out=out, in0=in0, scalar=scalar, in1=in1, op0=op0, op1=op1), reads, writes)

    def cp(self, out, in_, reads, writes, eng="dve"):
        self.S.add(eng, lambda e: e.tensor_copy(out=out, in_=in_), reads, writes)

    def recip(self, out, in_, reads, writes):
        self.S.add("dve", lambda e: e.reciprocal(out=out, in_=in_), reads, writes)

    def red(self, out, in_, reads, writes):
        self.S.add("dve", lambda e: e.tensor_reduce(out=out, in_=in_, axis=AX.X, op=ALU.add), reads, writes)

    def memset(self, ap, val, writes, eng="dve"):
        self.S.add(eng, lambda e: e.memset(ap, val), (), writes)

    def dma(self, q, out, in_, reads, writes):
        self.S.add(q, lambda e: e.dma_start(out=out, in_=in_), reads, writes, dma=True)

    def evac(self, out, in_, reads, writes):
        self.ev_i += 1
        mode = getattr(self, "ev_mode", "mix")
        if mode == "dve" or (mode == "mix" and self.ev_i % 3 == 0):
            self.cp(out, in_, reads, writes)
        else:
            self.act(out, in_, AF.Copy, reads, writes)


def build_program():
    K = KB()
    nc, S = K.nc, K.S
    _DEV['sched'] = S
    del _MARKS[:]
    mm, act, tt, ts, stt, cp, evac, dma = K.mm, K.act, K.tt, K.ts, K.stt, K.cp, K.evac, K.dma

    xin = K.din("xin", [2, T, 1024])
    cfa_d = K.din("cfa", [128, 7 * 128])
    cb_d = K.din("cb", [128, 256 + 1024 + 128])
    rope_d = K.din("rope", [128, 4, T])
    pcol_d = K.din("pcol", [128, 152])
    prow_d = K.din("prow", [128, 2, 400])
    ck_d = K.din("ck", [2, 256, 128])
    cv_d = K.din("cv", [2, 256, 128])
    cckv_d = K.din("cckv", [2, 256, 128])
    ckr_d = K.din("ckr", [2, 256, 32])
    cst_d = K.din("cst", [2, 2, 4, 64, 64])
    wmod_d = K.din("w_mod", [2, 1024, 6144])
    win_d = K.din("w_in", [2, 1024, 2336])
    wkvb_d = K.din("w_kv_b", [2, 128, 512])
    wout_d = K.din("w_out", [2, 1024, 1024])
    wup_d = K.din("w_up", [2, 1024, 4096])
    wdn_d = K.din("w_down", [2, 4096, 1024])
    y_d = K.dout("y", [2, T, 1024])
    nk_d = K.dout("nk", [4, 2, 256, 128])
    nv_d = K.dout("nv", [4, 2, 256, 128])
    nckv_d = K.dout("nckv", [4, 2, 256, 128])
    nkr_d = K.dout("nkr", [4, 2, 256, 32])
    nst_d = K.dout("nst", [4, 2, 2, 4, 64, 64])

    xT = K.sb("xT", [128, 8, T], F32)
    hT = K.sb("hT", [128, 8, T], BF16)
    mixT = K.sb("mixT", [128, 8, T], BF16)
    ring = [K.sb("ring%d" % i, [128, 8, 512], BF16) for i in range(NRING)]
    cfa = K.sb("cfa_sb", [128, 7 * 128], F32)
    cbt = K.sb("cb_sb", [128, 256 + 1024 + 128], BF16)
    rope = K.sb("rope_sb", [128, 4, T], BF16)
    pcol = K.sb("pcol_sb", [128, 152], F32)
    prow = K.sb("prow_sb", [128, 2, 400], F32)
    wkvb = K.sb("wkvb_sb", [128, 2, 512], BF16)
    silT = K.sb("silT", [128, 8, 2], BF16)
    modT = K.sb("modT", [128, 2, 48, 2], F32)
    g1 = K.sb("g1", [128, 2, 2, 8], F32)
    g2 = K.sb("g2", [128, 2, 2, 8], F32)
    small = K.sb("small", [128, 64], F32)
    lg = K.sb("lg", [128, 2, 8], F32)
    lgs = K.sb("lgs", [128, 2, 4], F32)
    Wt = K.sb("Wt", [128, 4, 128], F32)
    Et = K.sb("Et", [128, 4, 128], F32)
    dtab = K.sb("dtab", [128, 2, 8], F32)
    cdB = K.sb("cdB", [128, 2, 4, 64], F32)
    esk = K.sb("esk", [128, 2, 8], F32)
    esr = K.sb("esr", [1, 8], BF16)
    wtmp = K.sb("wtmp", [128, 2, 128], F32)
    xs = K.sb("xs", [128, 1, 1024], F32)
    tmpf = K.sb("tmpf", [128, 2, 512], F32)
    tmpg = K.sb("tmpg", [128, 2, 512], F32)
    rstd = K.sb("rstd", [128, 512], F32)
    qT = K.sb("qT", [128, 4, T], BF16)
    kaT = K.sb("kaT", [128, T + 256], BF16)
    vaug = K.sb("vaug", [128, 10, 256], BF16)
    PT = [K.sb("PT%d" % i, [128, 512], BF16) for i in range(3)]
    rec = K.sb("rec", [128, 1, 512], F32)
    kbT = K.sb("kbT", [128, 2, T], BF16)
    kdec = K.sb("kdec", [128, 2, 512], BF16)
    vb = K.sb("vb", [128, 8, 256], BF16)
    sgt = K.sb("sgt", [128, 2, 256], F32)
    Usb = K.sb("Usb", [128, 8, 256], F32)
    Srun = K.sb("Srun", [128, 256], F32)
    Sstb = K.sb("Sstb", [128, 8, 256], BF16)
    fst = K.sb("fst", [128, 256], F32)
    SWt = K.sb("SWt", [128, 2, 512], BF16)
    qdec = K.sb("qdec", [128, 2, 512], BF16)
    lnt = K.sb("lnt", [128, 2, 256], F32)
    lns = K.sb("lns", [128, 2, 16], F32)
    kch = K.sb("kch", [128, 2, T + 256], BF16)
    vch = K.sb("vch", [128, 2, 10, 128], BF16)
    ckvnT = K.sb("ckvnT", [128, T + 256], BF16)
    ckvf = K.sb("ckvf", [128, 2, 128], F32)
    ckvb = K.sb("ckvb", [128, 2, 128], BF16)
    ost = K.sb("ost", [128, 2, 288], F32)

    pb = [K.ps("pb%d" % i, [128, 512]) for i in range(8)]

    ident = cfa[:, 0:128]
    A1, M1, A2, M2, E1 = (cfa[:, 128 * i:128 * (i + 1)] for i in range(1, 6))
    cj = cfa[:, 768:896]
    identb = cbt[:, 0:128]
    onesb = cbt[:, 128:256]
    mask_lo = cbt[:, 256:768]
    mask_hi = cbt[:, 768:1280]
    zo = cbt[0:1, 1280:1408]

    def PB(i):
        return ("pb", i)

    def ring_next():
        i = K.ring_i % NRING
        K.ring_i += 1
        return i

    def wload(slot, segs, src):
        for (dc, sc, n) in segs:
            dma("pool", ring[slot][:, :, dc:dc + n], src[:, sc:sc + n].rearrange("(k p) n -> p k n", p=128),
                (), [("ring", slot)])

    dma("sp", cfa[:, :], cfa_d[:, :], (), ["cfa"])
    dma("pool", rope[:, :, :], rope_d[:, :, :], (), ["rope"])
    dma("sp", pcol[:, :], pcol_d[:, :], (), ["pcol"])
    dma("sp", prow[:, :, :], prow_d[:, :, :], (), ["prow"])
    dma("pool", cbt[:, :], cb_d[:, :], (), ["cb"])
    dma("pool", wkvb[:, :, :], wkvb_d.rearrange("l p n -> p l n"), (), ["wkvb"])
    K.memset(vaug[:, :, :], 1.0, [("vaug", i) for i in range(10)])
    K.memset(vch[:, :, :, :], 1.0, [("vch", 0), ("vch", 1)])
    K.memset(Sstb[:, :, :], 0.0, [("Sstb", i, d) for i in range(8) for d in range(2)])

    act(small[:, 0:16], pcol[:, 136:152], AF.Exp, ["pcol"], ["small"], scale=-1.0)
    ts(small[:, 0:16], small[:, 0:16], 1.0, None, ALU.add, None, ["small"], ["small"])
    K.recip(small[:, 0:16], small[:, 0:16], ["small"], ["small"])
    tt(silT[:, :, :].rearrange("p k j -> p j k"), small[:, 0:16].rearrange("p (j k) -> p j k", j=2),
       pcol[:, 136:152].rearrange("p (j k) -> p j k", j=2), ALU.mult, ["small", "pcol"], ["silT"])

    def norm_stats(tti, src_key_fn):
        sl = slice(tti * 512, (tti + 1) * 512)
        act(mixT[:, :, 0:512], xT[:, :, sl], AF.Square, [("x", k, tti) for k in range(8)],
            [("mix", k, 0) for k in range(8)])
        for k in range(8):
            mm(pb[7][:, :], onesb, mixT[:, k, 0:512], k == 0, k == 7, ["cb", ("mix", k, 0)], [PB(7)])
        act(rstd[:, :], pb[7][:, :], AF.Ln, [PB(7)], ["rstd"], bias=EPS, scale=1.0 / 1024.0)
        act(rstd[:, :], rstd[:, :], AF.Exp, ["rstd"], ["rstd"], scale=-0.5)

    def norm_mod(gcol, shcol, dst_fn):
        for tti in range(2):
            sl = slice(tti * 512, (tti + 1) * 512)
            norm_stats(tti, None)
            for k in range(8):
                r = k % 2
                stt(tmpf[:, r, :], xT[:, k, sl], gcol(k), rstd[:, :], ALU.mult, ALU.mult,
                    [("x", k, tti), "rstd", ("g1", 0), ("g1", 1), ("g2", 0), ("g2", 1), "pcol"], [("tmpf", r)])
                out, wkeys = dst_fn(k, tti)
                if shcol is None:
                    cp(out, tmpf[:, r, :], [("tmpf", r)], wkeys)
                else:
                    act(out, tmpf[:, r, :], AF.Identity, [("tmpf", r), ("modT", 0), ("modT", 1)], wkeys, bias=shcol(k), scale=1.0)

    bank_rr = [0]

    held = set()

    def nbank(lo=0, hi=5, hold=False):
        for _ in range(hi - lo):
            b = lo + bank_rr[0] % (hi - lo)
            bank_rr[0] += 1
            if b not in held:
                if hold:
                    held.add(b)
                return b
        raise RuntimeError("all PSUM banks in [%d,%d) are held" % (lo, hi))

    def release(b):
        held.discard(b)

    def proj_fm(slot, col0, M, kind_rows=128):
        for tti in range(2):
            b = nbank()
            for k in range(8):
                mm(pb[b][0:M, :], ring[slot][:, k, col0:col0 + M], hT[:, k, tti * 512:(tti + 1) * 512],
                   k == 0, k == 7, [("ring", slot), ("h", k, tti)], [PB(b)])
            yield tti, b

    def proj_tm(slot, col0, N, tt8, bank=None, hold=False):
        b = nbank(hold=hold) if bank is None else bank
        for k in range(8):
            mm(pb[b][:, 0:N], hT[:, k, tt8 * 128:(tt8 + 1) * 128], ring[slot][:, k, col0:col0 + N],
               k == 0, k == 7, [("ring", slot), ("h", k, tt8 // 4)], [PB(b)])
        return b

    def swap_copy(dst, src, cols, w):
        sv = ring[src][:, :, 0:cols].rearrange("p k (n two w) -> p k n two w", two=2, w=w)
        dv = ring[dst][:, :, 0:cols].rearrange("p k (n two w) -> p k n two w", two=2, w=w)
        for a in range(2):
            cp(dv[:, :, :, a, :], sv[:, :, :, 1 - a, :], [("ring", src)], [("ring", dst)])

    def rope_evac(out, bA, bB, rows, tti, tabc, tabs, okeys):
        sl = slice(tti * 512, (tti + 1) * 512)
        tt(tmpf[rows, 0, :], pb[bA][rows, :], rope[rows, tabc, sl], ALU.mult, [PB(bA), "rope"], [("tmpf", 0)])
        tt(tmpg[rows, 0, :], pb[bB][rows, :], rope[rows, tabs, sl], ALU.mult, [PB(bB), "rope"], [("tmpg", 0)])
        tt(out, tmpf[rows, 0, :], tmpg[rows, 0, :], ALU.add, [("tmpf", 0), ("tmpg", 0)], okeys)

    def interleave(gens):
        gens = list(gens)
        while gens:
            for g_ in list(gens):
                try:
                    next(g_)
                except StopIteration:
                    gens.remove(g_)

    def load_tile(kind, t8):
        if t8 % 2 == 1:
            stg, skey = xs[:, 0, :], [("xs", 0)]
        else:
            stg, skey = tmpf[:, :, :].rearrange("p a b -> p (a b)"), [("tmpf", 0), ("tmpf", 1)]
        dma("sp" if kind == 0 else "pool", stg, xin[kind, t8 * 128:(t8 + 1) * 128, :], (), skey)
        hb = 2 * (t8 % 2)
        hi, lo = qT[:, hb, :], qT[:, hb + 1, :]
        hk, lk = [("qT", hb, 0), ("qT", hb, 1)], [("qT", hb + 1, 0), ("qT", hb + 1, 1)]
        act(hi, stg, AF.Copy, skey, hk)
        yield
        tt(lo, stg, hi, ALU.subtract, skey + hk, lk)
        yield
        for half in range(2):
            b = nbank(hold=True)
            for kk in range(4):
                k = half * 4 + kk
                mm(pb[b][:, kk * 128:(kk + 1) * 128], hi[:, k * 128:(k + 1) * 128], identb, True, False,
                   hk + ["cb"], [PB(b)])
                mm(pb[b][:, kk * 128:(kk + 1) * 128], lo[:, k * 128:(k + 1) * 128], identb, False, True,
                   lk + ["cb"], [PB(b)])
                yield
            evac(xT[:, half * 4:half * 4 + 4, t8 * 128:(t8 + 1) * 128],
                 pb[b][:, :].rearrange("p (k t) -> p k t", k=4), [PB(b)],
                 [("x", half * 4 + kk, t8 // 4) for kk in range(4)])
            release(b)
            yield

    def load_x(kind):
        for t8 in range(0, 8, 2):
            interleave([load_tile(kind, t8), load_tile(kind, t8 + 1)])

    def mod_blocks(l, blks):
        for blk in blks:
            s_ = ring_next()
            wload(s_, [(0, blk * 512, 512)], wmod_d[l])
            b = nbank()
            for mi in range(4):
                for k in range(8):
                    mm(pb[b][:, 2 * mi:2 * mi + 2], ring[s_][:, k, mi * 128:(mi + 1) * 128], silT[:, k, :],
                       k == 0, k == 7, [("ring", s_), "silT"], [PB(b)])
            tt(modT[:, l, 4 * blk:4 * blk + 4, :], pb[b][:, 0:8].rearrange("p (c j) -> p c j", j=2),
               pcol[:, 40 + 48 * l + 4 * blk:44 + 48 * l + 4 * blk].unsqueeze(2).to_broadcast([128, 4, 2]), ALU.add,
               [PB(b), "pcol"], [("modT", l)])
        if 3 in blks:
            for j in range(2):
                stt(g1[:, l, j, :], modT[:, l, 8:16, j], 1.0, pcol[:, 8 * l:8 * l + 8], ALU.add, ALU.mult,
                    [("modT", l), "pcol"], [("g1", l)])
        if 9 in blks:
            for j in range(2):
                stt(g2[:, l, j, :], modT[:, l, 32:40, j], 1.0, pcol[:, 16 + 8 * l:24 + 8 * l], ALU.add, ALU.mult,
                    [("modT", l), "pcol"], [("g2", l)])

    def setup_mod():
        for l in range(2):
            act(lg[:, l, :], prow[:, l, 392:400], AF.Exp, ["prow"], ["lg"], scale=-1.0)
            act(lg[:, l, :], lg[:, l, :], AF.Ln, ["lg"], ["lg"], bias=1.0, scale=1.0)
            ts(lg[:, l, :], lg[:, l, :], -1.0, None, ALU.mult, None, ["lg"], ["lg"])
            cp(lgs[0:64, l, :], lg[0:64, l, 0:4], ["lg"], ["lgs"])
            cp(lgs[64:128, l, :], lg[64:128, l, 4:8], ["lg"], ["lgs"])
            act(esk[:, l, :], prow[:, l, 384:392], AF.Exp, ["prow"], ["esk"])
            act(dtab[:, l, 0:4], lg[:, l, 0:4], AF.Exp, ["lg", "cfa"], ["dtab"], scale=cj[:, 0:1])
            act(dtab[:, l, 4:8], lg[:, l, 4:8], AF.Exp, ["lg", "cfa"], ["dtab"], scale=cj[:, 1:2])
            ts(dtab[:, l, :], dtab[:, l, :], RET_K_SCALE, None, ALU.mult, None, ["dtab"], ["dtab"])
            act(small[:, 16:20], lgs[:, l, :], AF.Exp, ["lgs"], ["small"], scale=128.0)
            cp(cdB[:, l, :, :], small[:, 16:20].unsqueeze(2).to_broadcast([128, 4, 64]), ["small"], ["cdB"])


    def run_pass(kind):
        lat = kind == 1
        nseq = 1 if lat else 4
        nchunk = 8 if lat else 2
        nkt = 10 if lat else 8
        if kind == 1:
            load_x(kind)
        ck(2)
        for l in range(2):
            layer(kind, l)
            ck(10 + l)

        def fin_bufs(hf):
            if hf == 0:
                return (qT[:, 0:2, :].rearrange("p a (k t) -> p (a k) t", t=256),
                        qT[:, 2:4, :].rearrange("p a (k t) -> p (a k) t", t=256),
                        [("qT", c, j) for c in range(2) for j in range(2)],
                        [("qT", c, j) for c in range(2, 4) for j in range(2)])
            return (hT[:, 0:2, :].rearrange("p a (k t) -> p (a k) t", t=256),
                    hT[:, 2:4, :].rearrange("p a (k t) -> p (a k) t", t=256),
                    [("h", c, j) for c in range(2) for j in range(2)],
                    [("h", c, j) for c in range(2, 4) for j in range(2)])

        def fin_p1(tti, hf):
            if hf == 0:
                norm_stats(tti, None)
                yield
            sl = slice(tti * 512 + hf * 256, tti * 512 + (hf + 1) * 256)
            for k in range(8):
                stt(Usb[:, k, :], xT[:, k, sl], pcol[:, 32 + k:33 + k], rstd[:, hf * 256:(hf + 1) * 256],
                    ALU.mult, ALU.mult, [("x", k, tti), "rstd", "pcol"], [("Usb", k)])
                yield
            hiv, lov, hk, lk = fin_bufs(hf)
            uk = [("Usb", k) for k in range(8)]
            act(hiv, Usb[:, :, :], AF.Copy, uk, hk)
            yield
            tt(lov, Usb[:, :, :], hiv, ALU.subtract, uk + hk, lk)
            yield

        def fin_p2(tti, hf):
            hiv, lov, hk, lk = fin_bufs(hf)
            for t2 in range(2):
                t8 = tti * 4 + hf * 2 + t2
                for half in range(2):
                    b = nbank(hold=True)
                    for kk in range(4):
                        k = half * 4 + kk
                        mm(pb[b][:, kk * 128:(kk + 1) * 128], hiv[:, k, t2 * 128:(t2 + 1) * 128], identb, True, False,
                           hk + ["cb"], [PB(b)])
                        mm(pb[b][:, kk * 128:(kk + 1) * 128], lov[:, k, t2 * 128:(t2 + 1) * 128], identb, False, True,
                           lk + ["cb"], [PB(b)])
                        yield
                    if t8 % 2 == 0:
                        evac(xs[:, 0, half * 512:(half + 1) * 512], pb[b][:, :], [PB(b)], [("xs", 0)])
                    else:
                        evac(tmpg[:, half, :], pb[b][:, :], [PB(b)], [("tmpg", half)])
                    release(b)
                    yield
                if t8 % 2 == 0:
                    dma("sp", y_d[kind, t8 * 128:(t8 + 1) * 128, :], xs[:, 0, :], [("xs", 0)], ())
                else:
                    dma("sp", y_d[kind, t8 * 128:(t8 + 1) * 128, :], tmpg[:, :, :].rearrange("p a b -> p (a b)"),
                        [("tmpg", 0), ("tmpg", 1)], ())
                yield

        its = [(tti, hf) for tti in range(2) for hf in range(2)]
        interleave([fin_p1(*its[0])])
        for ii in range(len(its)):
            gl = [fin_p2(*its[ii])]
            if ii + 1 < len(its):
                gl.append(fin_p1(*its[ii + 1]))
            interleave(gl)

    def layer(kind, l):
        lat = kind == 1
        nseq = 1 if lat else 4
        nchunk = 8 if lat else 2
        sh1 = lambda k: modT[:, l, k, kind:kind + 1]
        gate1 = lambda k: modT[:, l, 16 + k, kind:kind + 1]
        sh2 = lambda k: modT[:, l, 24 + k, kind:kind + 1]
        gate2 = lambda k: modT[:, l, 40 + k, kind:kind + 1]
        win = win_d[l]

        if kind == 0 and (l == 0 or _DEV.get('nopref')):
            mod_blocks(l, [0, 1, 2, 3])
        norm_mod(lambda k: g1[:, l, kind, k:k + 1], sh1,
                 lambda k, tti: (hT[:, k, tti * 512:(tti + 1) * 512], [("h", k, tti)]))

        ck(3)
        if lat:
            for c2 in range(2):
                rows = slice(c2 * 128, (c2 + 1) * 128)
                o0 = c2 * 256
                dma("pool", qdec[:, 0, o0:o0 + 128], ck_d[l, rows, :], (), [("qdecp", 0, c2)])
                b = nbank()
                mm(pb[b][:, 0:128], qdec[:, 0, o0:o0 + 128], identb, True, True, [("qdecp", 0, c2), "cb"], [PB(b)])
                evac(kaT[:, T + c2 * 128:T + (c2 + 1) * 128], pb[b][:, 0:128], [PB(b)], [("kaT", 8 + c2)])
                dma("pool", vaug[:, 8 + c2, :].rearrange("p (g x) -> p g x", g=2)[:, :, 0:64],
                    cv_d[l, rows, :].rearrange("p (g d) -> p g d", g=2), (), [("vaug", 8 + c2)])
                dma("pool", qdec[:, 0, o0 + 128:o0 + 256], cckv_d[l, rows, :], (), [("qdecp", 1, c2)])
                b = nbank()
                mm(pb[b][:, 0:128], qdec[:, 0, o0 + 128:o0 + 256], identb, True, True, [("qdecp", 1, c2), "cb"], [PB(b)])
                evac(ckvnT[:, T + c2 * 128:T + (c2 + 1) * 128], pb[b][:, 0:128], [PB(b)], [("ckvnT", 8 + c2)])
                k0 = c2 * 96
                dma("pool", qdec[:, 1, k0 + 64:k0 + 96], ckr_d[l, rows, :], (), [("qdecp", 2, c2)])
                b = nbank()
                mm(pb[b][0:96, 0:128], qdec[:, 1, k0:k0 + 96], identb, True, True, [("qdecp", 2, c2), "cb"], [PB(b)])
                for r2 in range(2):
                    evac(kch[64:96, r2, T + c2 * 128:T + (c2 + 1) * 128], pb[b][64:96, 0:128], [PB(b)],
                         [("kchr", r2)])

        cp(esr[0:1, :], esk[0:1, l, :], ["esk"], ["esr"])
        sA1 = ring_next()
        wload(sA1, [(c * 128 + g * 64, (4 * g + c) * 64, 64) for c in range(4) for g in range(2)], win)
        sA2 = ring_next()
        wload(sA2, [(0, 512, 256)], win)
        if lat:
            sA1s = ring_next()
            swap_copy(sA1s, sA1, 512, 16)
            sA2s = ring_next()
            swap_copy(sA2s, sA2, 128, 16)
        ck(31)
        for c in range(4):
            if not lat:
                for tti, b in proj_fm(sA1, c * 128, 128):
                    evac(qT[:, c, tti * 512:(tti + 1) * 512], pb[b][:, :], [PB(b)], [("qT", c, tti)])
            else:
                ga = proj_fm(sA1, c * 128, 128)
                gb = proj_fm(sA1s, c * 128, 128)
                for (tti, bA), (_, bB) in zip(ga, gb):
                    rope_evac(qT[:, c, tti * 512:(tti + 1) * 512], bA, bB, slice(0, 128), tti, 0, 1, [("qT", c, tti)])
        ck(32)
        if not lat:
            for tti, b in proj_fm(sA2, 0, 128):
                evac(kaT[:, tti * 512:(tti + 1) * 512], pb[b][:, :], [PB(b)], [("kaT", tti * 4 + i) for i in range(4)])
        else:
            ga = proj_fm(sA2, 0, 128)
            gb = proj_fm(sA2s, 0, 128)
            for (tti, bA), (_, bB) in zip(ga, gb):
                rope_evac(kaT[:, tti * 512:(tti + 1) * 512], bA, bB, slice(0, 128), tti, 0, 1,
                          [("kaT", tti * 4 + i) for i in range(4)])
        ck(33)
        for t8 in range(8):
            b = proj_tm(sA2, 0, 256, t8)
            cp(vaug[:, t8, :].rearrange("p (g x) -> p g x", g=2)[:, :, 0:64],
               pb[b][:, 128:256].rearrange("p (g d) -> p g d", g=2), [PB(b)], [("vaug", t8)])
            if not lat and _DEV.get("x", 9) >= 1:
                r = t8 % 2
                act(ost[:, r, 0:256], pb[b][:, 0:256], AF.Copy, [PB(b)], [("ost", r, 0)])
                s_, tloc = t8 // 2, (t8 % 2) * 128
                if _DEV.get("x", 9) >= 2:
                    dma("sp", nk_d[s_, l, tloc:tloc + 128, :], ost[:, r, 0:128], [("ost", r, 0)], ())
                if _DEV.get("x", 9) >= 3:
                    dma("sp", nv_d[s_, l, tloc:tloc + 128, :], ost[:, r, 128:256], [("ost", r, 0)], ())
        ck(4)
        K.ev_mode = "dve"
        units = []
        for g in range(2):
            for qb in range(8):
                if lat:
                    kts = [(kt, (1 if kt == qb + 1 else (2 if kt == qb - 1 else 0)))
                           for kt in (qb - 1, qb, qb + 1) if 0 <= kt < 8] + [(8, 0), (9, 0)]
                else:
                    s0 = (qb // 2) * 2
                    kts = [(s0, 0), (s0 + 1, 0)]
                for ii, (kt, mk) in enumerate(kts):
                    units.append(dict(g=g, qb=qb, kt=kt, mk=mk, first=ii == 0, last=ii == len(kts) - 1,
                                      po=5 + (g * 8 + qb) % 3))

        def wa_s1(u):
            g, qb, kt, mk = u["g"], u["qb"], u["kt"], u["mk"]
            gs = slice(g * 64, (g + 1) * 64)
            b = nbank(hold=True)
            u["b"] = b
            mm(pb[b][:, :], kaT[gs, kt * 128:(kt + 1) * 128], qT[gs, :, qb * 128:(qb + 1) * 128], True, mk == 0,
               [("kaT", kt)] + [("qT", c, qb // 4) for c in range(4)], [PB(b)])
            if mk:
                mm(pb[b][:, :], identb, mask_lo if mk == 1 else mask_hi, False, True, ["cb"], [PB(b)])

        def wa_s23(u):
            g, qb, kt, b, po = u["g"], u["qb"], u["kt"], u["b"], u["po"]
            gs = slice(g * 64, (g + 1) * 64)
            r = K.uid % 3
            K.uid += 1
            act(PT[r][:, :], pb[b][:, :], AF.Exp, [PB(b)], [("PT", r)], scale=ATTN_SCALE)
            release(b)
            mm(pb[po][:, :], vaug[:, kt, g * 128:(g + 1) * 128], PT[r][:, :], u["first"], False,
               [("vaug", kt), ("PT", r)], [PB(po)])
            if u["last"]:
                mm(pb[po][:, :], zo, esr[0:1, 4 * g:4 * g + 4].unsqueeze(2).to_broadcast([1, 4, 128]), False, True,
                   ["cb", "esr"], [PB(po)])
                pend.append(u)

        def wa_fin(u):
            g, qb, po = u["g"], u["qb"], u["po"]
            gs = slice(g * 64, (g + 1) * 64)
            if True:
                act(rec[64:128, 0, :], pb[po][64:128, :], AF.Ln, [PB(po)], [("rec", 0)])
                act(rec[64:128, 0, :], rec[64:128, 0, :], AF.Exp, [("rec", 0)], [("rec", 0)], scale=-1.0)
                tt(mixT[gs, 0:4, qb * 128:(qb + 1) * 128], pb[po][0:64, :].rearrange("p (h q) -> p h q", h=4),
                   rec[64:128, 0, :].rearrange("p (h q) -> p h q", h=4), ALU.mult, [PB(po), ("rec", 0)],
                   [("mix", c, qb // 4) for c in range(4)])

        DEPTH = 3
        for i in range(min(DEPTH, len(units))):
            wa_s1(units[i])
        pend = []
        for i in range(len(units)):
            wa_s23(units[i])
            if i + DEPTH < len(units):
                wa_s1(units[i + DEPTH])
            if units[i]["last"] and len(pend) > 1:
                wa_fin(pend.pop(0))
        while pend:
            wa_fin(pend.pop(0))

        if kind == 0:
            mod_blocks(l, [4, 5, 6])
        ck(5)
        K.ev_mode = "act"
        for h in range(4):
            act(wtmp[:, 0, :], A1, AF.Exp, ["cfa", "lg"], ["wtmp0"], scale=lg[:, l, h:h + 1])
            tt(wtmp[:, 0, :], wtmp[:, 0, :], M1, ALU.mult, ["wtmp0", "cfa"], ["wtmp0"])
            act(wtmp[:, 1, :], A2, AF.Exp, ["cfa", "lg"], ["wtmp1"], scale=lg[:, l, 4 + h:5 + h])
            tt(wtmp[:, 1, :], wtmp[:, 1, :], M2, ALU.mult, ["wtmp1", "cfa"], ["wtmp1"])
            tt(wtmp[:, 0, :], wtmp[:, 0, :], wtmp[:, 1, :], ALU.add, ["wtmp0", "wtmp1"], ["wtmp0"])
            ts(Wt[:, h, :], wtmp[:, 0, :], RET_K_SCALE, None, ALU.mult, None, ["wtmp0"], ["Wt"])
            act(Et[:, h, :], E1, AF.Exp, ["cfa", "lgs"], ["Et"], scale=lgs[:, l, h:h + 1])
        ck(51)
        sBq = ring_next()
        wload(sBq, [(h * 128 + d * 64, 768 + h * 64, 64) for h in range(4) for d in range(2)], win)
        sBk = ring_next()
        wload(sBk, [(0, 1024, 512)], win)
        sBg = ring_next()
        wload(sBg, [(0, 1536, 256)], win)
        for h in range(4):
            for tti, b in proj_fm(sBq, h * 128, 128):
                evac(qT[:, h, tti * 512:(tti + 1) * 512], pb[b][:, :], [PB(b)], [("qT", h, tti)])
        for c in range(2):
            for tti, b in proj_fm(sBk, c * 128, 128):
                evac(kbT[:, c, tti * 512:(tti + 1) * 512], pb[b][:, :], [PB(b)], [("kbT", c, tti)])
        ck(52)
        def btm_s2(t8, b):
            r = t8 % 2
            kv = kdec[:, r, :].rearrange("p (h d e) -> p h d e", h=4, d=2)
            for d in range(2):
                tt(kv[:, :, d, :], pb[b][:, 0:256].rearrange("p (h e) -> p h e", h=4),
                   dtab[:, l, 4 * d:4 * d + 4].unsqueeze(2).to_broadcast([128, 4, 64]), ALU.mult,
                   [PB(b), "dtab"], [("kdec", r)])
            act(vb[:, t8, :], pb[b][:, 256:512], AF.Copy, [PB(b)], [("vb", t8)])
            bu = nbank()
            for h in range(4):
                mm(pb[bu][:, h * 64:(h + 1) * 64], kdec[:, r, h * 128:(h + 1) * 128], vb[:, t8, h * 64:(h + 1) * 64],
                   True, True, [("kdec", r), ("vb", t8)], [PB(bu)])
            evac(Usb[:, t8, :], pb[bu][:, 0:256], [PB(bu)], [("Usb", t8)])

        bq = [proj_tm(sBk, 0, 512, 0, hold=True)]
        for t8 in range(8):
            if t8 + 1 < 8:
                bq.append(proj_tm(sBk, 0, 512, t8 + 1, hold=True))
            btm_s2(t8, bq[t8])
            release(bq[t8])
        ck(53)
        F, Bk_ = slice(0, 64), slice(64, 128)
        cdv = cdB[:, l, :, :].rearrange("p h e -> p (h e)")
        if lat:
            dma("sp", Srun[F, :].rearrange("p (h e) -> p h e", h=4), cst_d[l, 0].rearrange("h d e -> d h e"),
                (), [("Srun", 0)])
            dma("sp", Srun[Bk_, :].rearrange("p (h e) -> p h e", h=4), cst_d[l, 1].rearrange("h d e -> d h e"),
                (), [("Srun", 1)])
            cp(Sstb[F, 0, :], Srun[F, :], [("Srun", 0)], [("Sstb", 0, 0)])
            for n in range(7):
                tt(Srun[F, :], Srun[F, :], cdv[F, :], ALU.mult, [("Srun", 0), "cdB"], [("Srun", 0)])
                tt(Srun[F, :], Srun[F, :], Usb[F, n, :], ALU.add, [("Srun", 0), ("Usb", n)], [("Srun", 0)])
                cp(Sstb[F, n + 1, :], Srun[F, :], [("Srun", 0)], [("Sstb", n + 1, 0)])
            cp(Sstb[Bk_, 7, :], Srun[Bk_, :], [("Srun", 1)], [("Sstb", 7, 1)])
            for n in range(7, 0, -1):
                tt(Srun[Bk_, :], Srun[Bk_, :], cdv[Bk_, :], ALU.mult, [("Srun", 1), "cdB"], [("Srun", 1)])
                tt(Srun[Bk_, :], Srun[Bk_, :], Usb[Bk_, n, :], ALU.add, [("Srun", 1), ("Usb", n)], [("Srun", 1)])
                cp(Sstb[Bk_, n - 1, :], Srun[Bk_, :], [("Srun", 1)], [("Sstb", n - 1, 1)])
        else:
            for s_ in range(4):
                t0, t1 = 2 * s_, 2 * s_ + 1
                K.memset(Sstb[F, t0, :], 0.0, [("Sstb", t0, 0)])
                cp(Sstb[Bk_, t0, :], Usb[Bk_, t1, :], [("Usb", t1)], [("Sstb", t0, 1)])
                K.memset(Sstb[Bk_, t1, :], 0.0, [("Sstb", t1, 1)])
                cp(Sstb[F, t1, :], Usb[F, t0, :], [("Usb", t0)], [("Sstb", t1, 0)])
                tt(fst[F, :], Usb[F, t0, :], cdv[F, :], ALU.mult, [("Usb", t0), "cdB"], ["fst"])
                tt(fst[F, :], fst[F, :], Usb[F, t1, :], ALU.add, ["fst", ("Usb", t1)], ["fst"])
                tt(fst[Bk_, :], Usb[Bk_, t1, :], cdv[Bk_, :], ALU.mult, [("Usb", t1), "cdB"], ["fst"])
                tt(fst[Bk_, :], fst[Bk_, :], Usb[Bk_, t0, :], ALU.add, ["fst", ("Usb", t0)], ["fst"])
                for d in range(2):
                    dma("sp", nst_d[s_, l, d].rearrange("h d e -> d h e"),
                        fst[d * 64:(d + 1) * 64, :].rearrange("p (h e) -> p h e", h=4), ["fst"], ())
        ck(54)
        sCk = ring_next()
        wload(sCk, [(0, 2176, 160)], win)
        if lat:
            sCks = ring_next()
            swap_copy(sCks, sCk, 160, 8)
        if not lat:
            for tti, b in proj_fm(sCk, 64, 96):
                for r2 in range(2):
                    evac(kch[64:96, r2, tti * 512:(tti + 1) * 512], pb[b][64:96, :], [PB(b)], [("kchr", r2)])
        else:
            ga = proj_fm(sCk, 64, 96)
            gb = proj_fm(sCks, 64, 96)
            for (tti, bA), (_, bB) in zip(ga, gb):
                sl = slice(tti * 512, (tti + 1) * 512)
                rope_evac(kch[64:96, 0, sl], bA, bB, slice(64, 96), tti, 2, 3, [("kchr", 0)])
                act(kch[64:96, 1, sl], kch[64:96, 0, sl], AF.Copy, [("kchr", 0)], [("kchr", 1)])
        def ctm_s2(t8, b):
            r = t8 % 2
            K.memset(lns[:, r, 12:13], 0.0, [("lnsc", r)])
            yield
            act(ckvf[:, r, :], pb[b][:, 0:128], AF.Square, [PB(b), ("lnsc", r)], [("ckvf", r), ("lnsc", r)],
                accum_out=lns[:, r, 12:13])
            yield
            act(lns[:, r, 12:13], lns[:, r, 12:13], AF.Ln, [("lnsc", r)], [("lnsc", r)], bias=EPS, scale=1.0 / 128.0)
            yield
            act(lns[:, r, 12:13], lns[:, r, 12:13], AF.Exp, [("lnsc", r)], [("lnsc", r)], scale=-0.5)
            yield
            stt(ckvf[:, r, :], pb[b][:, 0:128], lns[:, r, 12:13], prow[:, l, 256:384], ALU.mult, ALU.mult,
                [PB(b), ("lnsc", r), "prow", ("ckvf", r)], [("ckvf", r)])
            yield
            if not lat:
                s_, tloc = t8 // 2, (t8 % 2) * 128
                dma("sp", nckv_d[s_, l, tloc:tloc + 128, :], ckvf[:, r, :], [("ckvf", r)], ())
                yield
                act(ost[:, r, 256:288], pb[b][:, 128:160], AF.Copy, [PB(b)], [("ost", r, 1)])
                yield
                dma("sp", nkr_d[s_, l, tloc:tloc + 128, :], ost[:, r, 256:288], [("ost", r, 1)], ())
                yield
            act(ckvb[:, r, :], ckvf[:, r, :], AF.Copy, [("ckvf", r)], [("ckvb", r)])
            yield
            mm(pb[b][:, 256:384], ckvb[:, r, :], identb, True, True, [("ckvb", r), "cb"], [PB(b)])
            yield
            evac(ckvnT[:, t8 * 128:(t8 + 1) * 128], pb[b][:, 256:384], [PB(b)], [("ckvnT", t8)])
            yield
        cq = [proj_tm(sCk, 0, 160, 0, bank=5), proj_tm(sCk, 0, 160, 1, bank=6)]

        def c_step(t8):
            if t8 + 2 < 8:
                cq.append(proj_tm(sCk, 0, 160, t8 + 2, bank=5 + (t8 + 2) % 3))
            yield from ctm_s2(t8, cq[t8])

        def interleave(gens):
            gens = list(gens)
            while gens:
                for g_ in list(gens):
                    try:
                        next(g_)
                    except StopIteration:
                        gens.remove(g_)

        RU = [dict(t8=t8, r=t8 % 2) for t8 in range(8)]

        def rb_s1(u):
            t8 = u["t8"]
            sl8 = slice(t8 * 128, (t8 + 1) * 128)
            u["b2"] = proj_tm(sBg, 0, 256, t8, hold=True)
            bp = [nbank(hold=True), nbank(hold=True)]
            u["bp"] = bp
            for par in range(2):
                hs = slice(par * 64, par * 64 + 64)
                for hh in range(2):
                    h = 2 * hh + par
                    mm(pb[bp[par]][:, hh * 128:(hh + 1) * 128], kbT[hs, hh, sl8], qT[hs, h, sl8], True, True,
                       [("kbT", hh, t8 // 4), ("qT", h, t8 // 4)], [PB(bp[par])])

        def rb_s2(u):
            t8, r, b2, bp = u["t8"], u["r"], u["b2"], u["bp"]
            sl8 = slice(t8 * 128, (t8 + 1) * 128)
            act(sgt[:, r, :], pb[b2][:, 0:256], AF.Exp, [PB(b2)], [("sgt", r)], scale=-1.0)
            yield
            act(sgt[:, r, :], sgt[:, r, :], AF.Ln, [("sgt", r)], [("sgt", r)], bias=1.0, scale=1.0)
            yield
            act(sgt[:, r, :], sgt[:, r, :], AF.Exp, [("sgt", r)], [("sgt", r)], scale=-1.0)
            yield
            tt(sgt[:, r, :], sgt[:, r, :], prow[:, l, 0:256], ALU.mult, [("sgt", r), "prow"], [("sgt", r)])
            yield
            tt(sgt[:, r, :], pb[b2][:, 0:256], sgt[:, r, :], ALU.mult, [PB(b2), ("sgt", r)], [("sgt", r)])
            yield
            for par in range(2):
                tt(SWt[:, r, :].rearrange("p (hh two i) -> p two hh i", two=2, i=128)[:, par],
                   pb[bp[par]][:, 0:256].rearrange("p (hh i) -> p hh i", hh=2),
                   Wt[:, :, :].rearrange("p (hh two) i -> p two hh i", two=2)[:, par], ALU.mult,
                   [PB(bp[par]), "Wt"], [("SWt", r)])
                yield
            release(b2)
            release(bp[0])
            release(bp[1])
            tt(qdec[:, r, :].rearrange("p (h i) -> p h i", h=4), qT[:, :, sl8], Et[:, :, :], ALU.mult,
               [("qT", h, t8 // 4) for h in range(4)] + ["Et"],
               [("qdec", r)] + [("qdecp", i, c2) for i in range(3) for c2 in range(2)])
            yield

        def rb_s3(u):
            t8, r = u["t8"], u["r"]
            bo = nbank(hold=True)
            u["bo"] = bo
            for h in range(4):
                mm(pb[bo][:, h * 64:(h + 1) * 64], SWt[:, r, h * 128:(h + 1) * 128], vb[:, t8, h * 64:(h + 1) * 64],
                   True, False, [("SWt", r), ("vb", t8)], [PB(bo)])
                mm(pb[bo][:, h * 64:(h + 1) * 64], qdec[:, r, h * 128:(h + 1) * 128], Sstb[:, t8, h * 64:(h + 1) * 64],
                   False, True, [("qdec", r), ("Sstb", t8, 0), ("Sstb", t8, 1)], [PB(bo)])

        def rb_s4(u):
            r, bo = u["r"], u["bo"]
            o3 = pb[bo][:, 0:256].rearrange("p (h e) -> p h e", h=4)
            K.red(lns[:, r, 0:4], o3, [PB(bo)], [("lns", r)])
            yield
            act(lnt[:, r, :], pb[bo][:, 0:256], AF.Square, [PB(bo)], [("lnt", r)])
            yield
            K.red(lns[:, r, 4:8], lnt[:, r, :].rearrange("p (h e) -> p h e", h=4), [("lnt", r)], [("lns", r)])
            yield
            ts(lns[:, r, 0:4], lns[:, r, 0:4], 1.0 / 64.0, None, ALU.mult, None, [("lns", r)], [("lns", r)])
            yield
            tt(lns[:, r, 8:12], lns[:, r, 0:4], lns[:, r, 0:4], ALU.mult, [("lns", r)], [("lns", r)])
            yield
            stt(lns[:, r, 4:8], lns[:, r, 4:8], 1.0 / 64.0, lns[:, r, 8:12], ALU.mult, ALU.subtract,
                [("lns", r)], [("lns", r)])
            yield
            act(lns[:, r, 4:8], lns[:, r, 4:8], AF.Ln, [("lns", r)], [("lns", r)], bias=EPS, scale=1.0)
            yield
            act(lns[:, r, 4:8], lns[:, r, 4:8], AF.Exp, [("lns", r)], [("lns", r)], scale=-0.5)
            yield
            l3 = lnt[:, r, :].rearrange("p (h e) -> p h e", h=4)
            tt(l3, o3, lns[:, r, 0:4].unsqueeze(2).to_broadcast([128, 4, 64]), ALU.subtract, [PB(bo), ("lns", r)],
               [("lnt", r)])
            yield
            tt(l3, l3, lns[:, r, 4:8].unsqueeze(2).to_broadcast([128, 4, 64]), ALU.mult, [("lnt", r), ("lns", r)],
               [("lnt", r)])
            yield
            tt(SWt[:, r, 0:256], lnt[:, r, :], sgt[:, r, :], ALU.mult, [("lnt", r), ("sgt", r)], [("SWt", r)])
            yield
            release(bo)

        def rb_s5(u):
            t8, r = u["t8"], u["r"]
            sl8 = slice(t8 * 128, (t8 + 1) * 128)
            bt = nbank()
            for c in range(2):
                mm(pb[bt][:, c * 128:(c + 1) * 128], SWt[:, r, c * 128:(c + 1) * 128], identb, True, True,
                   [("SWt", r), "cb"], [PB(bt)])
            evac(mixT[:, 4:6, sl8], pb[bt][:, 0:256].rearrange("p (c t) -> p c t", c=2), [PB(bt)],
                 [("mix", 4, t8 // 4), ("mix", 5, t8 // 4)])

        rb_s1(RU[0])
        interleave([rb_s2(RU[0])])
        rb_s1(RU[1])
        for i in range(8):
            rb_s3(RU[i])
            if i >= 1:
                rb_s5(RU[i - 1])
            gl = [rb_s4(RU[i]), c_step(i)]
            if i + 1 < 8:
                gl.insert(0, rb_s2(RU[i + 1]))
            interleave(gl)
            if i + 2 < 8:
                rb_s1(RU[i + 2])
        rb_s5(RU[7])

        if kind == 0:
            mod_blocks(l, [7, 8, 9])
        ck(6)
        K.ev_mode = "dve"
        sCq = ring_next()
        wload(sCq, [(0, 1792, 384)], win)
        if lat:
            sCqs = ring_next()
            swap_copy(sCqs, sCq, 384, 8)
        for h in range(4):
            if not lat:
                for tti, b in proj_fm(sCq, h * 96, 96):
                    evac(qT[0:96, h, tti * 512:(tti + 1) * 512], pb[b][0:96, :], [PB(b)], [("qT", h, tti)])
            else:
                ga = proj_fm(sCq, h * 96, 96)
                gb = proj_fm(sCqs, h * 96, 96)
                for (tti, bA), (_, bB) in zip(ga, gb):
                    sl = slice(tti * 512, (tti + 1) * 512)
                    act(qT[0:64, h, sl], pb[bA][0:64, :], AF.Copy, [PB(bA)], [("qT", h, tti)])
                    rope_evac(qT[64:96, h, sl], bA, bB, slice(64, 96), tti, 2, 3, [("qT", h, tti)])
        nkt = 10 if lat else 8
        qtiles = [(0, 512), (512, 512)] if lat else [(s_ * 256, 256) for s_ in range(4)]
        built = set()

        def mla_build(h):
            if h in built:
                return
            built.add(h)
            r2 = h % 2
            for c0 in range(0, nkt * 128, 512):
                n = min(512, nkt * 128 - c0)
                b = nbank()
                mm(pb[b][0:64, 0:n], wkvb[:, l, h * 128:h * 128 + 64], ckvnT[:, c0:c0 + n], True, True,
                   ["wkvb"] + [("ckvnT", c0 // 128 + i) for i in range(n // 128)], [PB(b)])
                evac(kch[0:64, r2, c0:c0 + n], pb[b][0:64, 0:n], [PB(b)], [("kchn", r2)])
            for k0 in range(0, nkt, 8):
                nk_ = min(8, nkt - k0)
                b = nbank()
                for kk in range(nk_):
                    kt = k0 + kk
                    mm(pb[b][:, kk * 64:(kk + 1) * 64], ckvnT[:, kt * 128:(kt + 1) * 128],
                       wkvb[:, l, h * 128 + 64:h * 128 + 128], True, True, ["wkvb", ("ckvnT", kt)], [PB(b)])
                evac(vch[:, r2, k0:k0 + nk_, 0:64], pb[b][:, 0:nk_ * 64].rearrange("p (k e) -> p k e", e=64), [PB(b)],
                     [("vch", r2)])

        units = []
        for h in range(4):
            for qi, (q0, qn) in enumerate(qtiles):
                kts = list(range(10)) if lat else [2 * qi, 2 * qi + 1]
                for ii, kt in enumerate(kts):
                    units.append(dict(h=h, q0=q0, qn=qn, kt=kt, first=ii == 0, last=ii == len(kts) - 1,
                                      po=5 + (h * len(qtiles) + qi) % 3))

        def mc_s1(u):
            h, q0, qn, kt = u["h"], u["q0"], u["qn"], u["kt"]
            mla_build(h)
            r2 = h % 2
            b = nbank(hold=True)
            u["b"] = b
            mm(pb[b][:, 0:qn], kch[0:96, r2, kt * 128:(kt + 1) * 128], qT[0:96, h, q0:q0 + qn], True, True,
               [("kchn", r2), ("kchr", r2), ("qT", h, q0 // 512)], [PB(b)])

        def mc_s23(u):
            h, q0, qn, kt, b, po = u["h"], u["q0"], u["qn"], u["kt"], u["b"], u["po"]
            hs = slice((h % 2) * 64, (h % 2) * 64 + 64)
            r2 = h % 2
            r = K.uid % 3
            K.uid += 1
            act(PT[r][:, 0:qn], pb[b][:, 0:qn], AF.Exp, [PB(b)], [("PT", r)], scale=MLA_SCALE)
            release(b)
            mm(pb[po][:, 0:qn], vch[:, r2, kt, :], PT[r][:, 0:qn], u["first"], u["last"],
               [("vch", r2), ("PT", r)], [PB(po)])
            if u["last"]:
                pend.append(u)

        def mc_fin(u):
            h, q0, qn, po = u["h"], u["q0"], u["qn"], u["po"]
            hs = slice((h % 2) * 64, (h % 2) * 64 + 64)
            if True:
                act(rec[64:128, 0, 0:qn], pb[po][64:128, 0:qn], AF.Ln, [PB(po)], [("rec", 0)])
                act(rec[64:128, 0, 0:qn], rec[64:128, 0, 0:qn], AF.Exp, [("rec", 0)], [("rec", 0)], scale=-1.0)
                tt(mixT[hs, 6 + h // 2, q0:q0 + qn], pb[po][0:64, 0:qn], rec[64:128, 0, 0:qn], ALU.mult,
                   [PB(po), ("rec", 0)], [("mix", 6 + h // 2, q0 // 512)])

        for i in range(min(DEPTH, len(units))):
            mc_s1(units[i])
        pend = []
        for i in range(len(units)):
            mc_s23(units[i])
            if i + DEPTH < len(units):
                mc_s1(units[i + DEPTH])
            if units[i]["last"] and len(pend) > 1:
                mc_fin(pend.pop(0))
        while pend:
            mc_fin(pend.pop(0))

        ck(7)
        K.ev_mode = "mix"
        wo = wout_d[l]
        slots = []
        for half in range(2):
            s = ring_next()
            slots.append(s)
            cs = slice(half * 512, (half + 1) * 512)
            for ca in range(4):
                dma("pool", ring[s][0:64, ca, :], wo[64 * ca:64 * ca + 64, cs], (), [("ring", s)])
                dma("pool", ring[s][64:128, ca, :], wo[256 + 64 * ca:256 + 64 * ca + 64, cs], (), [("ring", s)])
            dma("pool", ring[s][:, 4:8, :], wo[512:1024, cs].rearrange("(k p) n -> p k n", p=128), (), [("ring", s)])
        for tti in range(2):
            sl = slice(tti * 512, (tti + 1) * 512)
            for m in range(8):
                b = nbank()
                s = slots[m // 4]
                for k in range(8):
                    mm(pb[b][:, :], ring[s][:, k, (m % 4) * 128:(m % 4 + 1) * 128], mixT[:, k, sl], k == 0, k == 7,
                       [("ring", s), ("mix", k, tti)], [PB(b)])
                stt(xT[:, m, sl], pb[b][:, :], gate1(m), xT[:, m, sl], ALU.mult, ALU.add,
                    [PB(b), ("modT", l), ("x", m, tti)], [("x", m, tti)])

        if kind == 0:
            mod_blocks(l, [10, 11])
        ck(8)
        norm_mod(lambda k: g2[:, l, kind, k:k + 1], sh2,
                 lambda k, tti: (hT[:, k, tti * 512:(tti + 1) * 512], [("h", k, tti)]))
        def load_mlp(fb):
            su = ring_next()
            wload(su, [(0, fb * 512, 512)], wup_d[l])
            sd = ring_next()
            for k4 in range(4):
                dma("pool", ring[sd][:, 2 * k4:2 * k4 + 2, :],
                    wdn_d[l, fb * 512 + k4 * 128:fb * 512 + (k4 + 1) * 128, :].rearrange("p (h n) -> p h n", h=2),
                    (), [("ring", sd)])
            return su, sd

        loaded = {}

        def mlp_up(fb, tti):
            if fb not in loaded:
                loaded[fb] = load_mlp(fb)
            su, sd = loaded[fb]
            sl = slice(tti * 512, (tti + 1) * 512)
            ab = (fb * 2 + tti) % 2
            for mi in range(4):
                b = nbank(0, 4)
                for k in range(8):
                    mm(pb[b][:, :], ring[su][:, k, mi * 128:(mi + 1) * 128], hT[:, k, sl], k == 0, k == 7,
                       [("ring", su), ("h", k, tti)], [PB(b)])
                r = mi % 2
                act(tmpg[:, r, :], pb[b][:, :], AF.Relu, [PB(b)], [("tmpg", r)])
                act(qT[:, mi, ab * 512:(ab + 1) * 512], tmpg[:, r, :], AF.Square, [("tmpg", r)], [("qT", mi, ab)])

        def mlp_down(fb, tti):
            su, sd = loaded[fb]
            sl = slice(tti * 512, (tti + 1) * 512)
            ab = (fb * 2 + tti) % 2
            for m in range(8):
                b = nbank(4, 8)
                for k4 in range(4):
                    mm(pb[b][:, :], ring[sd][:, 2 * k4 + m // 4, (m % 4) * 128:(m % 4 + 1) * 128],
                       qT[:, k4, ab * 512:(ab + 1) * 512], k4 == 0, k4 == 3, [("ring", sd), ("qT", k4, ab)], [PB(b)])
                stt(xT[:, m, sl], pb[b][:, :], gate2(m), xT[:, m, sl], ALU.mult, ALU.add,
                    [PB(b), ("modT", l), ("x", m, tti)], [("x", m, tti)])

        mu = [(fb, tti) for fb in range(8) for tti in range(2)]
        mlp_up(*mu[0])
        for i in range(len(mu)):
            if i + 1 < len(mu):
                mlp_up(*mu[i + 1])
            mlp_down(*mu[i])
            if kind == 0 and l == 0 and i in (2, 6, 10, 13) and not _DEV.get('nopref'):
                mod_blocks(1, [(2, 6, 10, 13).index(i)])

    try:
        load_x(0)
        setup_mod()
        ck(1)
        if not _DEV.get('skip0'):
            run_pass(0)
        ck(20)
        run_pass(1)
    except _Stop:
        pass

    names = S.sem_names()
    sems = {n: K.es.enter_context(nc.semaphore(n)) for n in names}
    block = K.es.enter_context(nc.Block())
    S.emit(block, sems)
    K.es.close()
    return nc


_CACHE = {}
_DEV = {}


class _Stop(Exception):
    pass


_MARKS = []


def ck(n):
    if _DEV.get('sched') is not None:
        _MARKS.append((n, _DEV['sched'].cnt['pe']))
    if _DEV.get('stop') == n:
        raise _Stop()


def kernel(x_prompt, x_sample, cache_win_k, cache_win_v, cache_mla_ckv, cache_mla_krope, state_ret,
           c, c_ctx, w_mod, b_mod, norm1, norm2, w_in, win_sink, ret_decay, ret_gn, mla_kv_norm,
           w_kv_b, w_out, w_up, w_down, final_norm):
    f = lambda a: np.ascontiguousarray(np.asarray(a, dtype=np.float32))
    x_prompt, x_sample = f(x_prompt), f(x_sample)
    cfa, cb, rope = host_consts()
    colT = lambda v: f(v).reshape(-1, 128).T
    prow = np.zeros((128, 2, 400), np.float32)
    for l in range(2):
        prow[:, l, 0:256] = f(ret_gn)[l][None, :]
        prow[:, l, 256:384] = f(mla_kv_norm)[l][None, :]
        prow[:, l, 384:392] = f(win_sink)[l][None, :]
        prow[:, l, 392:400] = f(ret_decay)[l].reshape(-1)[None, :]
    shared = {"cfa": cfa, "cb": cb, "rope": rope, "prow": prow,
              "w_mod": f(w_mod), "w_in": f(w_in), "w_kv_b": f(w_kv_b), "w_out": f(w_out),
              "w_up": f(w_up), "w_down": f(w_down)}
    in_maps = []
    for i in range(NCORES):
        pcol = np.concatenate([colT(f(norm1)[0]), colT(f(norm1)[1]), colT(f(norm2)[0]), colT(f(norm2)[1]),
                               colT(final_norm), colT(f(b_mod)[0]), colT(f(b_mod)[1]), colT(c_ctx),
                               colT(f(c)[i])], axis=1)
        m = dict(shared)
        m["pcol"] = f(pcol)
        m["xin"] = np.stack([x_prompt[4 * i:4 * i + 4].reshape(T, 1024), x_sample[i]], 0)
        m["ck"] = f(cache_win_k)[i].reshape(2, 256, 128)
        m["cv"] = f(cache_win_v)[i].reshape(2, 256, 128)
        m["cckv"] = f(cache_mla_ckv)[i]
        m["ckr"] = f(cache_mla_krope)[i]
        m["cst"] = f(state_ret)[i]
        in_maps.append(m)
    if _DEV.get("only_core0"):
        in_maps = in_maps[:1]
    if "nc" not in _CACHE:
        _CACHE["nc"] = build_program()
    res = run_bass_kernel_spmd(_CACHE["nc"], in_maps, core_ids=list(range(len(in_maps))))
    R = res.results
    y_prompt = np.concatenate([r["y"][0].reshape(4, 256, 1024) for r in R], 0)
    y_sample = np.stack([r["y"][1] for r in R], 0)
    nk = np.concatenate([r["nk"].reshape(4, 2, 256, 2, 64) for r in R], 0)
    nv = np.concatenate([r["nv"].reshape(4, 2, 256, 2, 64) for r in R], 0)
    nckv = np.concatenate([r["nckv"] for r in R], 0)
    nkr = np.concatenate([r["nkr"] for r in R], 0)
    nst = np.concatenate([r["nst"] for r in R], 0)
    return (y_prompt.astype(np.float32), y_sample.astype(np.float32), nk.astype(np.float32), nv.astype(np.float32),
            nckv.astype(np.float32), nkr.astype(np.float32), nst.astype(np.float32))
```

```python
import numpy as np
from contextlib import ExitStack
import concourse.bass as bass
import concourse.mybir as mybir
from concourse.bass_utils import run_bass_kernel_spmd

F32 = mybir.dt.float32
BF16 = mybir.dt.bfloat16
AF = mybir.ActivationFunctionType
ALU = mybir.AluOpType
AX = mybir.AxisListType

ENGS = ("pe", "act", "dve", "pool", "sp")
NDMASEM = 16
EPS = 1e-6
NCORES = 8
T = 1024
ATTN_SCALE = 64 ** -0.5
RET_K_SCALE = 64 ** -0.5
MLA_SCALE = 96 ** -0.5
NEG = -30000.0
NRING = 4


class Sched:
    def __init__(self, nc):
        self.nc = nc
        self.ops = {e: [] for e in ENGS}
        self.cnt = {e: 0 for e in ENGS}
        self.lastw = {}
        self.readers = {}
        self.seen = {e: {} for e in ENGS}
        self.dma_n = {e: 0 for e in ENGS}
        self.dma_tok = {e: [None] * NDMASEM for e in ENGS}
        self.last_pe = None

    def _need(self, eng, tok, waits):
        if tok is None:
            return
        sem, val, peng = tok
        if eng == "pe" and peng == "pe" and sem == "c_pe":
            return
        if self.seen[eng].get(sem, 0) >= val:
            return
        waits[sem] = max(waits.get(sem, 0), val)

    def add(self, eng, fn, reads=(), writes=(), dma=False, pe_rows=None):
        waits = {}
        if eng == "pe" and pe_rows is not None and self.last_pe is not None:
            (lo, hi), banks, ltok = self.last_pe
            if (pe_rows[1] <= lo or pe_rows[0] >= hi) and set(banks) & set(writes):
                sem, val, _ = ltok
                if self.seen[eng].get(sem, 0) < val:
                    waits[sem] = val
        for k in reads:
            self._need(eng, self.lastw.get(k), waits)
            if isinstance(k, tuple) and k[0] == "pb":
                for t in self.readers.get(k, ()):
                    if t[2] != eng:
                        self._need(eng, t, waits)
        for k in writes:
            self._need(eng, self.lastw.get(k), waits)
            for t in self.readers.get(k, ()):
                self._need(eng, t, waits)
        if dma:
            n = self.dma_n[eng]
            nd = 8 if eng == "pool" else NDMASEM
            slot = n % nd
            self._need(eng, self.dma_tok[eng][slot], waits)
            tok = ("dma_%s_%d" % (eng, slot), 16 * (n // nd + 1), eng)
            self.dma_tok[eng][slot] = tok
            self.dma_n[eng] = n + 1
        else:
            self.cnt[eng] += 1
            tok = ("c_" + eng, self.cnt[eng], eng)
        for s, v in waits.items():
            self.seen[eng][s] = max(self.seen[eng].get(s, 0), v)
        self.ops[eng].append((fn, list(waits.items()), tok, dma))
        for k in reads:
            self.readers.setdefault(k, []).append(tok)
        for k in writes:
            self.lastw[k] = tok
            self.readers[k] = []
        if eng == "pe":
            self.last_pe = (pe_rows if pe_rows is not None else (0, 128), list(writes), tok)
        return tok

    def sem_names(self):
        names = ["c_" + e for e in ENGS]
        for e in ENGS:
            names += ["dma_%s_%d" % (e, i) for i in range(min(NDMASEM, self.dma_n[e]))]
        return names

    def emit(self, block, sems):
        engobj = {"pe": "tensor", "act": "scalar", "dve": "vector", "pool": "gpsimd", "sp": "sync"}
        fin = {}
        for e in ENGS:
            if self.cnt[e] > 0:
                fin["c_" + e] = self.cnt[e]
            for t in self.dma_tok[e]:
                if t is not None:
                    fin[t[0]] = max(fin.get(t[0], 0), t[1])

        def make(e):
            def body(engine):
                for fn, waits, tok, dma in self.ops[e]:
                    for s, v in waits:
                        engine.wait_ge(sems[s], v)
                    fn(engine).then_inc(sems[tok[0]], 16 if dma else 1)
                if e == "sp":
                    for s, v in fin.items():
                        engine.wait_ge(sems[s], v)
            return body

        for e in ENGS:
            if self.ops[e] or e == "sp":
                getattr(block, engobj[e])(make(e))


def host_consts():
    j = np.arange(128, dtype=np.float32)[:, None]
    i = np.arange(128, dtype=np.float32)[None, :]
    cf = {}
    cf["ident"] = np.eye(128, dtype=np.float32)
    cf["A1"] = np.maximum(i - j, 0.0)
    cf["M1"] = (i >= j).astype(np.float32)
    cf["A2"] = np.maximum(j - i, 0.0)
    cf["M2"] = (j >= i).astype(np.float32)
    e1 = np.zeros((128, 128), np.float32)
    e1[:64] = i + 1.0
    e1[64:] = 128.0 - i
    cf["E1"] = e1
    cj = np.zeros((128, 128), np.float32)
    cj[:, 0] = 127.0 - j[:, 0]
    cj[:, 1] = j[:, 0]
    cf["cj"] = cj
    cfa = np.concatenate([cf[k] for k in ("ident", "A1", "M1", "A2", "M2", "E1", "cj")], axis=1)
    lo = np.where(j <= i, 0.0, NEG).astype(np.float32)
    hi = np.where(i <= j, 0.0, NEG).astype(np.float32)
    zo = np.zeros((128, 128), np.float32)
    zo[:, 64:] = 1.0
    cb = np.concatenate([np.eye(128, dtype=np.float32), np.ones((128, 128), np.float32),
                         np.tile(lo, (1, 4)), np.tile(hi, (1, 4)), zo], axis=1)
    t = np.arange(T)
    row = (t // 64).astype(np.float32)
    col = (t % 64).astype(np.float32)

    def tab(dim):
        nf = dim // 4
        inv = (10000.0 ** (-np.arange(nf, dtype=np.float32) / nf)).astype(np.float32)
        ar = row[None, :] * inv[:, None]
        ac = col[None, :] * inv[:, None]
        c = np.concatenate([np.cos(ar), np.cos(ar), np.cos(ac), np.cos(ac)], 0)
        s = np.concatenate([-np.sin(ar), np.sin(ar), -np.sin(ac), np.sin(ac)], 0)
        return c.astype(np.float32), s.astype(np.float32)

    ca, sa = tab(64)
    cm, sm = tab(32)
    rope = np.zeros((128, 4, T), np.float32)
    rope[:, 0] = np.tile(ca, (2, 1))
    rope[:, 1] = np.tile(sa, (2, 1))
    rope[64:96, 2] = cm
    rope[64:96, 3] = sm
    return cfa, cb, rope


def swap_pairs(a, w):
    n = a.shape[-1]
    return a.reshape(a.shape[:-1] + (n // (2 * w), 2, w))[..., ::-1, :].reshape(a.shape)


class KB:
    def __init__(self):
        self.nc = bass.Bass("TRN2", target_bir_lowering=False)
        self.S = Sched(self.nc)
        self.es = ExitStack()
        self.ring_i = 0
        self.ev_i = 0
        self.uid = 0

    def din(self, name, shape):
        return self.nc.dram_tensor(name, list(shape), F32, kind="ExternalInput").ap()

    def dout(self, name, shape):
        return self.nc.dram_tensor(name, list(shape), F32, kind="ExternalOutput").ap()

    def sb(self, name, shape, dt):
        return self.es.enter_context(self.nc.sbuf_tensor(name, list(shape), dt))

    def ps(self, name, shape, dt=F32):
        return self.es.enter_context(self.nc.psum_tensor(name, list(shape), dt))

    def mm(self, out, lhsT, rhs, start, stop, reads, writes):
        lo = lhsT.base_partition()
        rows = (lo - lo % 32, lo + ((lhsT.partition_size() + 31) // 32) * 32)
        self.S.add("pe", lambda e: e.matmul(out, lhsT, rhs, start=start, stop=stop), reads, writes, pe_rows=rows)

    def tr(self, out, in_, ident, reads, writes):
        self.S.add("pe", lambda e: e.transpose(out, in_, ident), reads, writes)

    def act(self, out, in_, func, reads, writes, **kw):
        self.S.add("act", lambda e: e.activation(out=out, in_=in_, func=func, **kw), reads, writes)

    def tt(self, out, in0, in1, op, reads, writes, eng="dve"):
        self.S.add(eng, lambda e: e.tensor_tensor(out=out, in0=in0, in1=in1, op=op), reads, writes)

    def ts(self, out, in0, s1, s2, op0, op1, reads, writes, eng="dve"):
        if s2 is None:
            self.S.add(eng, lambda e: e.tensor_scalar(out=out, in0=in0, scalar1=s1, scalar2=None, op0=op0), reads, writes)
        else:
            self.S.add(eng, lambda e: e.tensor_scalar(out=out, in0=in0, scalar1=s1, scalar2=s2, op0=op0, op1=op1), reads, writes)

    def stt(self, out, in0, scalar, in1, op0, op1, reads, writes, eng="dve"):
        self.S.add(eng, lambda e: e.scalar_tensor_tensor(out=out, in0=in0, scalar=scalar, in1=in1, op0=op0, op1=op1), reads, writes)

    def cp(self, out, in_, reads, writes, eng="dve"):
        self.S.add(eng, lambda e: e.tensor_copy(out=out, in_=in_), reads, writes)

    def recip(self, out, in_, reads, writes):
        self.S.add("dve", lambda e: e.reciprocal(out=out, in_=in_), reads, writes)

    def red(self, out, in_, reads, writes):
        self.S.add("dve", lambda e: e.tensor_reduce(out=out, in_=in_, axis=AX.X, op=ALU.add), reads, writes)

    def memset(self, ap, val, writes, eng="dve"):
        self.S.add(eng, lambda e: e.memset(ap, val), (), writes)

    def dma(self, q, out, in_, reads, writes):
        self.S.add(q, lambda e: e.dma_start(out=out, in_=in_), reads, writes, dma=True)

    def evac(self, out, in_, reads, writes):
        self.ev_i += 1
        mode = getattr(self, "ev_mode", "mix")
        if mode == "dve" or (mode == "mix" and self.ev_i % 3 == 0):
            self.cp(out, in_, reads, writes)
        else:
            self.act(out, in_, AF.Copy, reads, writes)


def build_program():
    K = KB()
    nc, S = K.nc, K.S
    _DEV['sched'] = S
    del _MARKS[:]
    mm, act, tt, ts, stt, cp, evac, dma = K.mm, K.act, K.tt, K.ts, K.stt, K.cp, K.evac, K.dma

    xin = K.din("xin", [2, T, 1024])
    cfa_d = K.din("cfa", [128, 7 * 128])
    cb_d = K.din("cb", [128, 256 + 1024 + 128])
    rope_d = K.din("rope", [128, 4, T])
    pcol_d = K.din("pcol", [128, 152])
    prow_d = K.din("prow", [128, 2, 400])
    ck_d = K.din("ck", [2, 256, 128])
    cv_d = K.din("cv", [2, 256, 128])
    cckv_d = K.din("cckv", [2, 256, 128])
    ckr_d = K.din("ckr", [2, 256, 32])
    cst_d = K.din("cst", [2, 2, 4, 64, 64])
    wmod_d = K.din("w_mod", [2, 1024, 6144])
    win_d = K.din("w_in", [2, 1024, 2336])
    wkvb_d = K.din("w_kv_b", [2, 128, 512])
    wout_d = K.din("w_out", [2, 1024, 1024])
    wup_d = K.din("w_up", [2, 1024, 4096])
    wdn_d = K.din("w_down", [2, 4096, 1024])
    y_d = K.dout("y", [2, T, 1024])
    nk_d = K.dout("nk", [4, 2, 256, 128])
    nv_d = K.dout("nv", [4, 2, 256, 128])
    nckv_d = K.dout("nckv", [4, 2, 256, 128])
    nkr_d = K.dout("nkr", [4, 2, 256, 32])
    nst_d = K.dout("nst", [4, 2, 2, 4, 64, 64])

    xT = K.sb("xT", [128, 8, T], F32)
    hT = K.sb("hT", [128, 8, T], BF16)
    mixT = K.sb("mixT", [128, 8, T], BF16)
    ring = [K.sb("ring%d" % i, [128, 8, 512], BF16) for i in range(NRING)]
    cfa = K.sb("cfa_sb", [128, 7 * 128], F32)
    cbt = K.sb("cb_sb", [128, 256 + 1024 + 128], BF16)
    rope = K.sb("rope_sb", [128, 4, T], BF16)
    pcol = K.sb("pcol_sb", [128, 152], F32)
    prow = K.sb("prow_sb", [128, 2, 400], F32)
    wkvb = K.sb("wkvb_sb", [128, 2, 512], BF16)
    silT = K.sb("silT", [128, 8, 2], BF16)
    modT = K.sb("modT", [128, 2, 48, 2], F32)
    g1 = K.sb("g1", [128, 2, 2, 8], F32)
    g2 = K.sb("g2", [128, 2, 2, 8], F32)
    small = K.sb("small", [128, 64], F32)
    lg = K.sb("lg", [128, 2, 8], F32)
    lgs = K.sb("lgs", [128, 2, 4], F32)
    Wt = K.sb("Wt", [128, 4, 128], F32)
    Et = K.sb("Et", [128, 4, 128], F32)
    dtab = K.sb("dtab", [128, 2, 8], F32)
    cdB = K.sb("cdB", [128, 2, 4, 64], F32)
    esk = K.sb("esk", [128, 2, 8], F32)
    esr = K.sb("esr", [1, 8], BF16)
    wtmp = K.sb("wtmp", [128, 2, 128], F32)
    xs = K.sb("xs", [128, 1, 1024], F32)
    tmpf = K.sb("tmpf", [128, 2, 512], F32)
    tmpg = K.sb("tmpg", [128, 2, 512], F32)
    rstd = K.sb("rstd", [128, 512], F32)
    qT = K.sb("qT", [128, 4, T], BF16)
    kaT = K.sb("kaT", [128, T + 256], BF16)
    vaug = K.sb("vaug", [128, 10, 256], BF16)
    PT = [K.sb("PT%d" % i, [128, 512], BF16) for i in range(3)]
    rec = K.sb("rec", [128, 1, 512], F32)
    kbT = K.sb("kbT", [128, 2, T], BF16)
    kdec = K.sb("kdec", [128, 2, 512], BF16)
    vb = K.sb("vb", [128, 8, 256], BF16)
    sgt = K.sb("sgt", [128, 2, 256], F32)
    Usb = K.sb("Usb", [128, 8, 256], F32)
    Srun = K.sb("Srun", [128, 256], F32)
    Sstb = K.sb("Sstb", [128, 8, 256], BF16)
    fst = K.sb("fst", [128, 256], F32)
    SWt = K.sb("SWt", [128, 2, 512], BF16)
    qdec = K.sb("qdec", [128, 2, 512], BF16)
    lnt = K.sb("lnt", [128, 2, 256], F32)
    lns = K.sb("lns", [128, 2, 16], F32)
    kch = K.sb("kch", [128, 2, T + 256], BF16)
    vch = K.sb("vch", [128, 2, 10, 128], BF16)
    ckvnT = K.sb("ckvnT", [128, T + 256], BF16)
    ckvf = K.sb("ckvf", [128, 2, 128], F32)
    ckvb = K.sb("ckvb", [128, 2, 128], BF16)
    ost = K.sb("ost", [128, 2, 288], F32)

    pb = [K.ps("pb%d" % i, [128, 512]) for i in range(8)]

    ident = cfa[:, 0:128]
    A1, M1, A2, M2, E1 = (cfa[:, 128 * i:128 * (i + 1)] for i in range(1, 6))
    cj = cfa[:, 768:896]
    identb = cbt[:, 0:128]
    onesb = cbt[:, 128:256]
    mask_lo = cbt[:, 256:768]
    mask_hi = cbt[:, 768:1280]
    zo = cbt[0:1, 1280:1408]

    def PB(i):
        return ("pb", i)

    def ring_next():
        i = K.ring_i % NRING
        K.ring_i += 1
        return i

    def wload(slot, segs, src):
        for (dc, sc, n) in segs:
            dma("pool", ring[slot][:, :, dc:dc + n], src[:, sc:sc + n].rearrange("(k p) n -> p k n", p=128),
                (), [("ring", slot)])

    dma("sp", cfa[:, :], cfa_d[:, :], (), ["cfa"])
    dma("pool", rope[:, :, :], rope_d[:, :, :], (), ["rope"])
    dma("sp", pcol[:, :], pcol_d[:, :], (), ["pcol"])
    dma("sp", prow[:, :, :], prow_d[:, :, :], (), ["prow"])
    dma("pool", cbt[:, :], cb_d[:, :], (), ["cb"])
    dma("pool", wkvb[:, :, :], wkvb_d.rearrange("l p n -> p l n"), (), ["wkvb"])
    K.memset(vaug[:, :, :], 1.0, [("vaug", i) for i in range(10)])
    K.memset(vch[:, :, :, :], 1.0, [("vch", 0), ("vch", 1)])
    K.memset(Sstb[:, :, :], 0.0, [("Sstb", i, d) for i in range(8) for d in range(2)])

    act(small[:, 0:16], pcol[:, 136:152], AF.Exp, ["pcol"], ["small"], scale=-1.0)
    ts(small[:, 0:16], small[:, 0:16], 1.0, None, ALU.add, None, ["small"], ["small"])
    K.recip(small[:, 0:16], small[:, 0:16], ["small"], ["small"])
    tt(silT[:, :, :].rearrange("p k j -> p j k"), small[:, 0:16].rearrange("p (j k) -> p j k", j=2),
       pcol[:, 136:152].rearrange("p (j k) -> p j k", j=2), ALU.mult, ["small", "pcol"], ["silT"])

    def norm_stats(tti, src_key_fn):
        sl = slice(tti * 512, (tti + 1) * 512)
        act(mixT[:, :, 0:512], xT[:, :, sl], AF.Square, [("x", k, tti) for k in range(8)],
            [("mix", k, 0) for k in range(8)])
        for k in range(8):
            mm(pb[7][:, :], onesb, mixT[:, k, 0:512], k == 0, k == 7, ["cb", ("mix", k, 0)], [PB(7)])
        act(rstd[:, :], pb[7][:, :], AF.Ln, [PB(7)], ["rstd"], bias=EPS, scale=1.0 / 1024.0)
        act(rstd[:, :], rstd[:, :], AF.Exp, ["rstd"], ["rstd"], scale=-0.5)

    def norm_mod(gcol, shcol, dst_fn):
        for tti in range(2):
            sl = slice(tti * 512, (tti + 1) * 512)
            norm_stats(tti, None)
            for k in range(8):
                r = k % 2
                stt(tmpf[:, r, :], xT[:, k, sl], gcol(k), rstd[:, :], ALU.mult, ALU.mult,
                    [("x", k, tti), "rstd", ("g1", 0), ("g1", 1), ("g2", 0), ("g2", 1), "pcol"], [("tmpf", r)])
                out, wkeys = dst_fn(k, tti)
                if shcol is None:
                    cp(out, tmpf[:, r, :], [("tmpf", r)], wkeys)
                else:
                    act(out, tmpf[:, r, :], AF.Identity, [("tmpf", r), ("modT", 0), ("modT", 1)], wkeys, bias=shcol(k), scale=1.0)

    bank_rr = [0]

    held = set()

    def nbank(lo=0, hi=5, hold=False):
        for _ in range(hi - lo):
            b = lo + bank_rr[0] % (hi - lo)
            bank_rr[0] += 1
            if b not in held:
                if hold:
                    held.add(b)
                return b
        raise RuntimeError("all PSUM banks in [%d,%d) are held" % (lo, hi))

    def release(b):
        held.discard(b)

    def proj_fm(slot, col0, M, kind_rows=128):
        for tti in range(2):
            b = nbank()
            for k in range(8):
                mm(pb[b][0:M, :], ring[slot][:, k, col0:col0 + M], hT[:, k, tti * 512:(tti + 1) * 512],
                   k == 0, k == 7, [("ring", slot), ("h", k, tti)], [PB(b)])
            yield tti, b

    def proj_tm(slot, col0, N, tt8, bank=None, hold=False):
        b = nbank(hold=hold) if bank is None else bank
        for k in range(8):
            mm(pb[b][:, 0:N], hT[:, k, tt8 * 128:(tt8 + 1) * 128], ring[slot][:, k, col0:col0 + N],
               k == 0, k == 7, [("ring", slot), ("h", k, tt8 // 4)], [PB(b)])
        return b

    def swap_copy(dst, src, cols, w):
        sv = ring[src][:, :, 0:cols].rearrange("p k (n two w) -> p k n two w", two=2, w=w)
        dv = ring[dst][:, :, 0:cols].rearrange("p k (n two w) -> p k n two w", two=2, w=w)
        for a in range(2):
            cp(dv[:, :, :, a, :], sv[:, :, :, 1 - a, :], [("ring", src)], [("ring", dst)])

    def rope_evac(out, bA, bB, rows, tti, tabc, tabs, okeys):
        sl = slice(tti * 512, (tti + 1) * 512)
        tt(tmpf[rows, 0, :], pb[bA][rows, :], rope[rows, tabc, sl], ALU.mult, [PB(bA), "rope"], [("tmpf", 0)])
        tt(tmpg[rows, 0, :], pb[bB][rows, :], rope[rows, tabs, sl], ALU.mult, [PB(bB), "rope"], [("tmpg", 0)])
        tt(out, tmpf[rows, 0, :], tmpg[rows, 0, :], ALU.add, [("tmpf", 0), ("tmpg", 0)], okeys)

    def interleave(gens):
        gens = list(gens)
        while gens:
            for g_ in list(gens):
                try:
                    next(g_)
                except StopIteration:
                    gens.remove(g_)

    def load_tile(kind, t8):
        if t8 % 2 == 1:
            stg, skey = xs[:, 0, :], [("xs", 0)]
        else:
            stg, skey = tmpf[:, :, :].rearrange("p a b -> p (a b)"), [("tmpf", 0), ("tmpf", 1)]
        dma("sp" if kind == 0 else "pool", stg, xin[kind, t8 * 128:(t8 + 1) * 128, :], (), skey)
        hb = 2 * (t8 % 2)
        hi, lo = qT[:, hb, :], qT[:, hb + 1, :]
        hk, lk = [("qT", hb, 0), ("qT", hb, 1)], [("qT", hb + 1, 0), ("qT", hb + 1, 1)]
        act(hi, stg, AF.Copy, skey, hk)
        yield
        tt(lo, stg, hi, ALU.subtract, skey + hk, lk)
        yield
        for half in range(2):
            b = nbank(hold=True)
            for kk in range(4):
                k = half * 4 + kk
                mm(pb[b][:, kk * 128:(kk + 1) * 128], hi[:, k * 128:(k + 1) * 128], identb, True, False,
                   hk + ["cb"], [PB(b)])
                mm(pb[b][:, kk * 128:(kk + 1) * 128], lo[:, k * 128:(k + 1) * 128], identb, False, True,
                   lk + ["cb"], [PB(b)])
                yield
            evac(xT[:, half * 4:half * 4 + 4, t8 * 128:(t8 + 1) * 128],
                 pb[b][:, :].rearrange("p (k t) -> p k t", k=4), [PB(b)],
                 [("x", half * 4 + kk, t8 // 4) for kk in range(4)])
            release(b)
            yield

    def load_x(kind):
        for t8 in range(0, 8, 2):
            interleave([load_tile(kind, t8), load_tile(kind, t8 + 1)])

    def mod_blocks(l, blks):
        for blk in blks:
            s_ = ring_next()
            wload(s_, [(0, blk * 512, 512)], wmod_d[l])
            b = nbank()
            for mi in range(4):
                for k in range(8):
                    mm(pb[b][:, 2 * mi:2 * mi + 2], ring[s_][:, k, mi * 128:(mi + 1) * 128], silT[:, k, :],
                       k == 0, k == 7, [("ring", s_), "silT"], [PB(b)])
            tt(modT[:, l, 4 * blk:4 * blk + 4, :], pb[b][:, 0:8].rearrange("p (c j) -> p c j", j=2),
               pcol[:, 40 + 48 * l + 4 * blk:44 + 48 * l + 4 * blk].unsqueeze(2).to_broadcast([128, 4, 2]), ALU.add,
               [PB(b), "pcol"], [("modT", l)])
        if 3 in blks:
            for j in range(2):
                stt(g1[:, l, j, :], modT[:, l, 8:16, j], 1.0, pcol[:, 8 * l:8 * l + 8], ALU.add, ALU.mult,
                    [("modT", l), "pcol"], [("g1", l)])
        if 9 in blks:
            for j in range(2):
                stt(g2[:, l, j, :], modT[:, l, 32:40, j], 1.0, pcol[:, 16 + 8 * l:24 + 8 * l], ALU.add, ALU.mult,
                    [("modT", l), "pcol"], [("g2", l)])

    def setup_mod():
        for l in range(2):
            act(lg[:, l, :], prow[:, l, 392:400], AF.Exp, ["prow"], ["lg"], scale=-1.0)
            act(lg[:, l, :], lg[:, l, :], AF.Ln, ["lg"], ["lg"], bias=1.0, scale=1.0)
            ts(lg[:, l, :], lg[:, l, :], -1.0, None, ALU.mult, None, ["lg"], ["lg"])
            cp(lgs[0:64, l, :], lg[0:64, l, 0:4], ["lg"], ["lgs"])
            cp(lgs[64:128, l, :], lg[64:128, l, 4:8], ["lg"], ["lgs"])
            act(esk[:, l, :], prow[:, l, 384:392], AF.Exp, ["prow"], ["esk"])
            act(dtab[:, l, 0:4], lg[:, l, 0:4], AF.Exp, ["lg", "cfa"], ["dtab"], scale=cj[:, 0:1])
            act(dtab[:, l, 4:8], lg[:, l, 4:8], AF.Exp, ["lg", "cfa"], ["dtab"], scale=cj[:, 1:2])
            ts(dtab[:, l, :], dtab[:, l, :], RET_K_SCALE, None, ALU.mult, None, ["dtab"], ["dtab"])
            act(small[:, 16:20], lgs[:, l, :], AF.Exp, ["lgs"], ["small"], scale=128.0)
            cp(cdB[:, l, :, :], small[:, 16:20].unsqueeze(2).to_broadcast([128, 4, 64]), ["small"], ["cdB"])


    def run_pass(kind):
        lat = kind == 1
        nseq = 1 if lat else 4
        nchunk = 8 if lat else 2
        nkt = 10 if lat else 8
        if kind == 1:
            load_x(kind)
        ck(2)
        for l in range(2):
            layer(kind, l)
            ck(10 + l)

        def fin_bufs(hf):
            if hf == 0:
                return (qT[:, 0:2, :].rearrange("p a (k t) -> p (a k) t", t=256),
                        qT[:, 2:4, :].rearrange("p a (k t) -> p (a k) t", t=256),
                        [("qT", c, j) for c in range(2) for j in range(2)],
                        [("qT", c, j) for c in range(2, 4) for j in range(2)])
            return (hT[:, 0:2, :].rearrange("p a (k t) -> p (a k) t", t=256),
                    hT[:, 2:4, :].rearrange("p a (k t) -> p (a k) t", t=256),
                    [("h", c, j) for c in range(2) for j in range(2)],
                    [("h", c, j) for c in range(2, 4) for j in range(2)])

        def fin_p1(tti, hf):
            if hf == 0:
                norm_stats(tti, None)
                yield
            sl = slice(tti * 512 + hf * 256, tti * 512 + (hf + 1) * 256)
            for k in range(8):
                stt(Usb[:, k, :], xT[:, k, sl], pcol[:, 32 + k:33 + k], rstd[:, hf * 256:(hf + 1) * 256],
                    ALU.mult, ALU.mult, [("x", k, tti), "rstd", "pcol"], [("Usb", k)])
                yield
            hiv, lov, hk, lk = fin_bufs(hf)
            uk = [("Usb", k) for k in range(8)]
            act(hiv, Usb[:, :, :], AF.Copy, uk, hk)
            yield
            tt(lov, Usb[:, :, :], hiv, ALU.subtract, uk + hk, lk)
            yield

        def fin_p2(tti, hf):
            hiv, lov, hk, lk = fin_bufs(hf)
            for t2 in range(2):
                t8 = tti * 4 + hf * 2 + t2
                for half in range(2):
                    b = nbank(hold=True)
                    for kk in range(4):
                        k = half * 4 + kk
                        mm(pb[b][:, kk * 128:(kk + 1) * 128], hiv[:, k, t2 * 128:(t2 + 1) * 128], identb, True, False,
                           hk + ["cb"], [PB(b)])
                        mm(pb[b][:, kk * 128:(kk + 1) * 128], lov[:, k, t2 * 128:(t2 + 1) * 128], identb, False, True,
                           lk + ["cb"], [PB(b)])
                        yield
                    if t8 % 2 == 0:
                        evac(xs[:, 0, half * 512:(half + 1) * 512], pb[b][:, :], [PB(b)], [("xs", 0)])
                    else:
                        evac(tmpg[:, half, :], pb[b][:, :], [PB(b)], [("tmpg", half)])
                    release(b)
                    yield
                if t8 % 2 == 0:
                    dma("sp", y_d[kind, t8 * 128:(t8 + 1) * 128, :], xs[:, 0, :], [("xs", 0)], ())
                else:
                    dma("sp", y_d[kind, t8 * 128:(t8 + 1) * 128, :], tmpg[:, :, :].rearrange("p a b -> p (a b)"),
                        [("tmpg", 0), ("tmpg", 1)], ())
                yield

        its = [(tti, hf) for tti in range(2) for hf in range(2)]
        interleave([fin_p1(*its[0])])
        for ii in range(len(its)):
            gl = [fin_p2(*its[ii])]
            if ii + 1 < len(its):
                gl.append(fin_p1(*its[ii + 1]))
            interleave(gl)

    def layer(kind, l):
        lat = kind == 1
        nseq = 1 if lat else 4
        nchunk = 8 if lat else 2
        sh1 = lambda k: modT[:, l, k, kind:kind + 1]
        gate1 = lambda k: modT[:, l, 16 + k, kind:kind + 1]
        sh2 = lambda k: modT[:, l, 24 + k, kind:kind + 1]
        gate2 = lambda k: modT[:, l, 40 + k, kind:kind + 1]
        win = win_d[l]

        if kind == 0 and (l == 0 or _DEV.get('nopref')):
            mod_blocks(l, [0, 1, 2, 3])
        norm_mod(lambda k: g1[:, l, kind, k:k + 1], sh1,
                 lambda k, tti: (hT[:, k, tti * 512:(tti + 1) * 512], [("h", k, tti)]))

        ck(3)
        if lat:
            for c2 in range(2):
                rows = slice(c2 * 128, (c2 + 1) * 128)
                o0 = c2 * 256
                dma("pool", qdec[:, 0, o0:o0 + 128], ck_d[l, rows, :], (), [("qdecp", 0, c2)])
                b = nbank()
                mm(pb[b][:, 0:128], qdec[:, 0, o0:o0 + 128], identb, True, True, [("qdecp", 0, c2), "cb"], [PB(b)])
                evac(kaT[:, T + c2 * 128:T + (c2 + 1) * 128], pb[b][:, 0:128], [PB(b)], [("kaT", 8 + c2)])
                dma("pool", vaug[:, 8 + c2, :].rearrange("p (g x) -> p g x", g=2)[:, :, 0:64],
                    cv_d[l, rows, :].rearrange("p (g d) -> p g d", g=2), (), [("vaug", 8 + c2)])
                dma("pool", qdec[:, 0, o0 + 128:o0 + 256], cckv_d[l, rows, :], (), [("qdecp", 1, c2)])
                b = nbank()
                mm(pb[b][:, 0:128], qdec[:, 0, o0 + 128:o0 + 256], identb, True, True, [("qdecp", 1, c2), "cb"], [PB(b)])
                evac(ckvnT[:, T + c2 * 128:T + (c2 + 1) * 128], pb[b][:, 0:128], [PB(b)], [("ckvnT", 8 + c2)])
                k0 = c2 * 96
                dma("pool", qdec[:, 1, k0 + 64:k0 + 96], ckr_d[l, rows, :], (), [("qdecp", 2, c2)])
                b = nbank()
                mm(pb[b][0:96, 0:128], qdec[:, 1, k0:k0 + 96], identb, True, True, [("qdecp", 2, c2), "cb"], [PB(b)])
                for r2 in range(2):
                    evac(kch[64:96, r2, T + c2 * 128:T + (c2 + 1) * 128], pb[b][64:96, 0:128], [PB(b)],
                         [("kchr", r2)])

        cp(esr[0:1, :], esk[0:1, l, :], ["esk"], ["esr"])
        sA1 = ring_next()
        wload(sA1, [(c * 128 + g * 64, (4 * g + c) * 64, 64) for c in range(4) for g in range(2)], win)
        sA2 = ring_next()
        wload(sA2, [(0, 512, 256)], win)
        if lat:
            sA1s = ring_next()
            swap_copy(sA1s, sA1, 512, 16)
            sA2s = ring_next()
            swap_copy(sA2s, sA2, 128, 16)
        ck(31)
        for c in range(4):
            if not lat:
                for tti, b in proj_fm(sA1, c * 128, 128):
                    evac(qT[:, c, tti * 512:(tti + 1) * 512], pb[b][:, :], [PB(b)], [("qT", c, tti)])
            else:
                ga = proj_fm(sA1, c * 128, 128)
                gb = proj_fm(sA1s, c * 128, 128)
                for (tti, bA), (_, bB) in zip(ga, gb):
                    rope_evac(qT[:, c, tti * 512:(tti + 1) * 512], bA, bB, slice(0, 128), tti, 0, 1, [("qT", c, tti)])
        ck(32)
        if not lat:
            for tti, b in proj_fm(sA2, 0, 128):
                evac(kaT[:, tti * 512:(tti + 1) * 512], pb[b][:, :], [PB(b)], [("kaT", tti * 4 + i) for i in range(4)])
        else:
            ga = proj_fm(sA2, 0, 128)
            gb = proj_fm(sA2s, 0, 128)
            for (tti, bA), (_, bB) in zip(ga, gb):
                rope_evac(kaT[:, tti * 512:(tti + 1) * 512], bA, bB, slice(0, 128), tti, 0, 1,
                          [("kaT", tti * 4 + i) for i in range(4)])
        ck(33)
        for t8 in range(8):
            b = proj_tm(sA2, 0, 256, t8)
            cp(vaug[:, t8, :].rearrange("p (g x) -> p g x", g=2)[:, :, 0:64],
               pb[b][:, 128:256].rearrange("p (g d) -> p g d", g=2), [PB(b)], [("vaug", t8)])
            if not lat and _DEV.get("x", 9) >= 1:
                r = t8 % 2
                act(ost[:, r, 0:256], pb[b][:, 0:256], AF.Copy, [PB(b)], [("ost", r, 0)])
                s_, tloc = t8 // 2, (t8 % 2) * 128
                if _DEV.get("x", 9) >= 2:
                    dma("sp", nk_d[s_, l, tloc:tloc + 128, :], ost[:, r, 0:128], [("ost", r, 0)], ())
                if _DEV.get("x", 9) >= 3:
                    dma("sp", nv_d[s_, l, tloc:tloc + 128, :], ost[:, r, 128:256], [("ost", r, 0)], ())
        ck(4)
        K.ev_mode = "dve"
        units = []
        for g in range(2):
            for qb in range(8):
                if lat:
                    kts = [(kt, (1 if kt == qb + 1 else (2 if kt == qb - 1 else 0)))
                           for kt in (qb - 1, qb, qb + 1) if 0 <= kt < 8] + [(8, 0), (9, 0)]
                else:
                    s0 = (qb // 2) * 2
                    kts = [(s0, 0), (s0 + 1, 0)]
                for ii, (kt, mk) in enumerate(kts):
                    units.append(dict(g=g, qb=qb, kt=kt, mk=mk, first=ii == 0, last=ii == len(kts) - 1,
                                      po=5 + (g * 8 + qb) % 3))

        def wa_s1(u):
            g, qb, kt, mk = u["g"], u["qb"], u["kt"], u["mk"]
            gs = slice(g * 64, (g + 1) * 64)
            b = nbank(hold=True)
            u["b"] = b
            mm(pb[b][:, :], kaT[gs, kt * 128:(kt + 1) * 128], qT[gs, :, qb * 128:(qb + 1) * 128], True, mk == 0,
               [("kaT", kt)] + [("qT", c, qb // 4) for c in range(4)], [PB(b)])
            if mk:
                mm(pb[b][:, :], identb, mask_lo if mk == 1 else mask_hi, False, True, ["cb"], [PB(b)])

        def wa_s23(u):
            g, qb, kt, b, po = u["g"], u["qb"], u["kt"], u["b"], u["po"]
            gs = slice(g * 64, (g + 1) * 64)
            r = K.uid % 3
            K.uid += 1
            act(PT[r][:, :], pb[b][:, :], AF.Exp, [PB(b)], [("PT", r)], scale=ATTN_SCALE)
            release(b)
            mm(pb[po][:, :], vaug[:, kt, g * 128:(g + 1) * 128], PT[r][:, :], u["first"], False,
               [("vaug", kt), ("PT", r)], [PB(po)])
            if u["last"]:
                mm(pb[po][:, :], zo, esr[0:1, 4 * g:4 * g + 4].unsqueeze(2).to_broadcast([1, 4, 128]), False, True,
                   ["cb", "esr"], [PB(po)])
                pend.append(u)

        def wa_fin(u):
            g, qb, po = u["g"], u["qb"], u["po"]
            gs = slice(g * 64, (g + 1) * 64)
            if True:
                act(rec[64:128, 0, :], pb[po][64:128, :], AF.Ln, [PB(po)], [("rec", 0)])
                act(rec[64:128, 0, :], rec[64:128, 0, :], AF.Exp, [("rec", 0)], [("rec", 0)], scale=-1.0)
                tt(mixT[gs, 0:4, qb * 128:(qb + 1) * 128], pb[po][0:64, :].rearrange("p (h q) -> p h q", h=4),
                   rec[64:128, 0, :].rearrange("p (h q) -> p h q", h=4), ALU.mult, [PB(po), ("rec", 0)],
                   [("mix", c, qb // 4) for c in range(4)])

        DEPTH = 4
        for i in range(min(DEPTH, len(units))):
            wa_s1(units[i])
        pend = []
        for i in range(len(units)):
            wa_s23(units[i])
            if i + DEPTH < len(units):
                wa_s1(units[i + DEPTH])
            if units[i]["last"] and len(pend) > 1:
                wa_fin(pend.pop(0))
        while pend:
            wa_fin(pend.pop(0))

        if kind == 0:
            mod_blocks(l, [4, 5, 6])
        ck(5)
        K.ev_mode = "act"
        for h in range(4):
            act(wtmp[:, 0, :], A1, AF.Exp, ["cfa", "lg"], ["wtmp0"], scale=lg[:, l, h:h + 1])
            tt(wtmp[:, 0, :], wtmp[:, 0, :], M1, ALU.mult, ["wtmp0", "cfa"], ["wtmp0"])
            act(wtmp[:, 1, :], A2, AF.Exp, ["cfa", "lg"], ["wtmp1"], scale=lg[:, l, 4 + h:5 + h])
            tt(wtmp[:, 1, :], wtmp[:, 1, :], M2, ALU.mult, ["wtmp1", "cfa"], ["wtmp1"])
            tt(wtmp[:, 0, :], wtmp[:, 0, :], wtmp[:, 1, :], ALU.add, ["wtmp0", "wtmp1"], ["wtmp0"])
            ts(Wt[:, h, :], wtmp[:, 0, :], RET_K_SCALE, None, ALU.mult, None, ["wtmp0"], ["Wt"])
            act(Et[:, h, :], E1, AF.Exp, ["cfa", "lgs"], ["Et"], scale=lgs[:, l, h:h + 1])
        ck(51)
        sBq = ring_next()
        wload(sBq, [(h * 128 + d * 64, 768 + h * 64, 64) for h in range(4) for d in range(2)], win)
        sBk = ring_next()
        wload(sBk, [(0, 1024, 512)], win)
        sBg = ring_next()
        wload(sBg, [(0, 1536, 256)], win)
        for h in range(4):
            for tti, b in proj_fm(sBq, h * 128, 128):
                evac(qT[:, h, tti * 512:(tti + 1) * 512], pb[b][:, :], [PB(b)], [("qT", h, tti)])
        for c in range(2):
            for tti, b in proj_fm(sBk, c * 128, 128):
                evac(kbT[:, c, tti * 512:(tti + 1) * 512], pb[b][:, :], [PB(b)], [("kbT", c, tti)])
        ck(52)
        def btm_s2(t8, b):
            r = t8 % 2
            kv = kdec[:, r, :].rearrange("p (h d e) -> p h d e", h=4, d=2)
            for d in range(2):
                tt(kv[:, :, d, :], pb[b][:, 0:256].rearrange("p (h e) -> p h e", h=4),
                   dtab[:, l, 4 * d:4 * d + 4].unsqueeze(2).to_broadcast([128, 4, 64]), ALU.mult,
                   [PB(b), "dtab"], [("kdec", r)])
            act(vb[:, t8, :], pb[b][:, 256:512], AF.Copy, [PB(b)], [("vb", t8)])
            bu = nbank()
            for h in range(4):
                mm(pb[bu][:, h * 64:(h + 1) * 64], kdec[:, r, h * 128:(h + 1) * 128], vb[:, t8, h * 64:(h + 1) * 64],
                   True, True, [("kdec", r), ("vb", t8)], [PB(bu)])
            evac(Usb[:, t8, :], pb[bu][:, 0:256], [PB(bu)], [("Usb", t8)])

        bq = [proj_tm(sBk, 0, 512, 0, hold=True)]
        for t8 in range(8):
            if t8 + 1 < 8:
                bq.append(proj_tm(sBk, 0, 512, t8 + 1, hold=True))
            btm_s2(t8, bq[t8])
            release(bq[t8])
        ck(53)
        F, Bk_ = slice(0, 64), slice(64, 128)
        cdv = cdB[:, l, :, :].rearrange("p h e -> p (h e)")
        if lat:
            dma("sp", Srun[F, :].rearrange("p (h e) -> p h e", h=4), cst_d[l, 0].rearrange("h d e -> d h e"),
                (), [("Srun", 0)])
            dma("sp", Srun[Bk_, :].rearrange("p (h e) -> p h e", h=4), cst_d[l, 1].rearrange("h d e -> d h e"),
                (), [("Srun", 1)])
            cp(Sstb[F, 0, :], Srun[F, :], [("Srun", 0)], [("Sstb", 0, 0)])
            for n in range(7):
                tt(Srun[F, :], Srun[F, :], cdv[F, :], ALU.mult, [("Srun", 0), "cdB"], [("Srun", 0)])
                tt(Srun[F, :], Srun[F, :], Usb[F, n, :], ALU.add, [("Srun", 0), ("Usb", n)], [("Srun", 0)])
                cp(Sstb[F, n + 1, :], Srun[F, :], [("Srun", 0)], [("Sstb", n + 1, 0)])
            cp(Sstb[Bk_, 7, :], Srun[Bk_, :], [("Srun", 1)], [("Sstb", 7, 1)])
            for n in range(7, 0, -1):
                tt(Srun[Bk_, :], Srun[Bk_, :], cdv[Bk_, :], ALU.mult, [("Srun", 1), "cdB"], [("Srun", 1)])
                tt(Srun[Bk_, :], Srun[Bk_, :], Usb[Bk_, n, :], ALU.add, [("Srun", 1), ("Usb", n)], [("Srun", 1)])
                cp(Sstb[Bk_, n - 1, :], Srun[Bk_, :], [("Srun", 1)], [("Sstb", n - 1, 1)])
        else:
            for s_ in range(4):
                t0, t1 = 2 * s_, 2 * s_ + 1
                K.memset(Sstb[F, t0, :], 0.0, [("Sstb", t0, 0)])
                cp(Sstb[Bk_, t0, :], Usb[Bk_, t1, :], [("Usb", t1)], [("Sstb", t0, 1)])
                K.memset(Sstb[Bk_, t1, :], 0.0, [("Sstb", t1, 1)])
                cp(Sstb[F, t1, :], Usb[F, t0, :], [("Usb", t0)], [("Sstb", t1, 0)])
                tt(fst[F, :], Usb[F, t0, :], cdv[F, :], ALU.mult, [("Usb", t0), "cdB"], ["fst"])
                tt(fst[F, :], fst[F, :], Usb[F, t1, :], ALU.add, ["fst", ("Usb", t1)], ["fst"])
                tt(fst[Bk_, :], Usb[Bk_, t1, :], cdv[Bk_, :], ALU.mult, [("Usb", t1), "cdB"], ["fst"])
                tt(fst[Bk_, :], fst[Bk_, :], Usb[Bk_, t0, :], ALU.add, ["fst", ("Usb", t0)], ["fst"])
                for d in range(2):
                    dma("sp", nst_d[s_, l, d].rearrange("h d e -> d h e"),
                        fst[d * 64:(d + 1) * 64, :].rearrange("p (h e) -> p h e", h=4), ["fst"], ())
        ck(54)
        sCk = ring_next()
        wload(sCk, [(0, 2176, 160)], win)
        if lat:
            sCks = ring_next()
            swap_copy(sCks, sCk, 160, 8)
        if not lat:
            for tti, b in proj_fm(sCk, 64, 96):
                for r2 in range(2):
                    evac(kch[64:96, r2, tti * 512:(tti + 1) * 512], pb[b][64:96, :], [PB(b)], [("kchr", r2)])
        else:
            ga = proj_fm(sCk, 64, 96)
            gb = proj_fm(sCks, 64, 96)
            for (tti, bA), (_, bB) in zip(ga, gb):
                sl = slice(tti * 512, (tti + 1) * 512)
                rope_evac(kch[64:96, 0, sl], bA, bB, slice(64, 96), tti, 2, 3, [("kchr", 0)])
                act(kch[64:96, 1, sl], kch[64:96, 0, sl], AF.Copy, [("kchr", 0)], [("kchr", 1)])
        def ctm_s2(t8, b):
            r = t8 % 2
            K.memset(lns[:, r, 12:13], 0.0, [("lnsc", r)])
            yield
            act(ckvf[:, r, :], pb[b][:, 0:128], AF.Square, [PB(b), ("lnsc", r)], [("ckvf", r), ("lnsc", r)],
                accum_out=lns[:, r, 12:13])
            yield
            act(lns[:, r, 12:13], lns[:, r, 12:13], AF.Ln, [("lnsc", r)], [("lnsc", r)], bias=EPS, scale=1.0 / 128.0)
            yield
            act(lns[:, r, 12:13], lns[:, r, 12:13], AF.Exp, [("lnsc", r)], [("lnsc", r)], scale=-0.5)
            yield
            stt(ckvf[:, r, :], pb[b][:, 0:128], lns[:, r, 12:13], prow[:, l, 256:384], ALU.mult, ALU.mult,
                [PB(b), ("lnsc", r), "prow", ("ckvf", r)], [("ckvf", r)])
            yield
            if not lat:
                s_, tloc = t8 // 2, (t8 % 2) * 128
                dma("sp", nckv_d[s_, l, tloc:tloc + 128, :], ckvf[:, r, :], [("ckvf", r)], ())
                yield
                act(ost[:, r, 256:288], pb[b][:, 128:160], AF.Copy, [PB(b)], [("ost", r, 1)])
                yield
                dma("sp", nkr_d[s_, l, tloc:tloc + 128, :], ost[:, r, 256:288], [("ost", r, 1)], ())
                yield
            act(ckvb[:, r, :], ckvf[:, r, :], AF.Copy, [("ckvf", r)], [("ckvb", r)])
            yield
            mm(pb[b][:, 256:384], ckvb[:, r, :], identb, True, True, [("ckvb", r), "cb"], [PB(b)])
            yield
            evac(ckvnT[:, t8 * 128:(t8 + 1) * 128], pb[b][:, 256:384], [PB(b)], [("ckvnT", t8)])
            yield
        cq = [proj_tm(sCk, 0, 160, 0, bank=5), proj_tm(sCk, 0, 160, 1, bank=6)]

        def c_step(t8):
            if t8 + 2 < 8:
                cq.append(proj_tm(sCk, 0, 160, t8 + 2, bank=5 + (t8 + 2) % 3))
            yield from ctm_s2(t8, cq[t8])

        def interleave(gens):
            gens = list(gens)
            while gens:
                for g_ in list(gens):
                    try:
                        next(g_)
                    except StopIteration:
                        gens.remove(g_)

        RU = [dict(t8=t8, r=t8 % 2) for t8 in range(8)]

        def rb_s1(u):
            t8 = u["t8"]
            sl8 = slice(t8 * 128, (t8 + 1) * 128)
            u["b2"] = proj_tm(sBg, 0, 256, t8, hold=True)
            bp = [nbank(hold=True), nbank(hold=True)]
            u["bp"] = bp
            for par in range(2):
                hs = slice(par * 64, par * 64 + 64)
                for hh in range(2):
                    h = 2 * hh + par
                    mm(pb[bp[par]][:, hh * 128:(hh + 1) * 128], kbT[hs, hh, sl8], qT[hs, h, sl8], True, True,
                       [("kbT", hh, t8 // 4), ("qT", h, t8 // 4)], [PB(bp[par])])

        def rb_s2(u):
            t8, r, b2, bp = u["t8"], u["r"], u["b2"], u["bp"]
            sl8 = slice(t8 * 128, (t8 + 1) * 128)
            act(sgt[:, r, :], pb[b2][:, 0:256], AF.Exp, [PB(b2)], [("sgt", r)], scale=-1.0)
            yield
            act(sgt[:, r, :], sgt[:, r, :], AF.Ln, [("sgt", r)], [("sgt", r)], bias=1.0, scale=1.0)
            yield
            act(sgt[:, r, :], sgt[:, r, :], AF.Exp, [("sgt", r)], [("sgt", r)], scale=-1.0)
            yield
            tt(sgt[:, r, :], sgt[:, r, :], prow[:, l, 0:256], ALU.mult, [("sgt", r), "prow"], [("sgt", r)])
            yield
            tt(sgt[:, r, :], pb[b2][:, 0:256], sgt[:, r, :], ALU.mult, [PB(b2), ("sgt", r)], [("sgt", r)])
            yield
            for par in range(2):
                tt(SWt[:, r, :].rearrange("p (hh two i) -> p two hh i", two=2, i=128)[:, par],
                   pb[bp[par]][:, 0:256].rearrange("p (hh i) -> p hh i", hh=2),
                   Wt[:, :, :].rearrange("p (hh two) i -> p two hh i", two=2)[:, par], ALU.mult,
                   [PB(bp[par]), "Wt"], [("SWt", r)])
                yield
            release(b2)
            release(bp[0])
            release(bp[1])
            tt(qdec[:, r, :].rearrange("p (h i) -> p h i", h=4), qT[:, :, sl8], Et[:, :, :], ALU.mult,
               [("qT", h, t8 // 4) for h in range(4)] + ["Et"],
               [("qdec", r)] + [("qdecp", i, c2) for i in range(3) for c2 in range(2)])
            yield

        def rb_s3(u):
            t8, r = u["t8"], u["r"]
            bo = nbank(hold=True)
            u["bo"] = bo
            for h in range(4):
                mm(pb[bo][:, h * 64:(h + 1) * 64], SWt[:, r, h * 128:(h + 1) * 128], vb[:, t8, h * 64:(h + 1) * 64],
                   True, False, [("SWt", r), ("vb", t8)], [PB(bo)])
                mm(pb[bo][:, h * 64:(h + 1) * 64], qdec[:, r, h * 128:(h + 1) * 128], Sstb[:, t8, h * 64:(h + 1) * 64],
                   False, True, [("qdec", r), ("Sstb", t8, 0), ("Sstb", t8, 1)], [PB(bo)])

        def rb_s4(u):
            r, bo = u["r"], u["bo"]
            o3 = pb[bo][:, 0:256].rearrange("p (h e) -> p h e", h=4)
            K.red(lns[:, r, 0:4], o3, [PB(bo)], [("lns", r)])
            yield
            act(lnt[:, r, :], pb[bo][:, 0:256], AF.Square, [PB(bo)], [("lnt", r)])
            yield
            K.red(lns[:, r, 4:8], lnt[:, r, :].rearrange("p (h e) -> p h e", h=4), [("lnt", r)], [("lns", r)])
            yield
            ts(lns[:, r, 0:4], lns[:, r, 0:4], 1.0 / 64.0, None, ALU.mult, None, [("lns", r)], [("lns", r)])
            yield
            tt(lns[:, r, 8:12], lns[:, r, 0:4], lns[:, r, 0:4], ALU.mult, [("lns", r)], [("lns", r)])
            yield
            stt(lns[:, r, 4:8], lns[:, r, 4:8], 1.0 / 64.0, lns[:, r, 8:12], ALU.mult, ALU.subtract,
                [("lns", r)], [("lns", r)])
            yield
            act(lns[:, r, 4:8], lns[:, r, 4:8], AF.Ln, [("lns", r)], [("lns", r)], bias=EPS, scale=1.0)
            yield
            act(lns[:, r, 4:8], lns[:, r, 4:8], AF.Exp, [("lns", r)], [("lns", r)], scale=-0.5)
            yield
            l3 = lnt[:, r, :].rearrange("p (h e) -> p h e", h=4)
            tt(l3, o3, lns[:, r, 0:4].unsqueeze(2).to_broadcast([128, 4, 64]), ALU.subtract, [PB(bo), ("lns", r)],
               [("lnt", r)])
            yield
            tt(l3, l3, lns[:, r, 4:8].unsqueeze(2).to_broadcast([128, 4, 64]), ALU.mult, [("lnt", r), ("lns", r)],
               [("lnt", r)])
            yield
            tt(SWt[:, r, 0:256], lnt[:, r, :], sgt[:, r, :], ALU.mult, [("lnt", r), ("sgt", r)], [("SWt", r)])
            yield
            release(bo)

        def rb_s5(u):
            t8, r = u["t8"], u["r"]
            sl8 = slice(t8 * 128, (t8 + 1) * 128)
            bt = nbank()
            for c in range(2):
                mm(pb[bt][:, c * 128:(c + 1) * 128], SWt[:, r, c * 128:(c + 1) * 128], identb, True, True,
                   [("SWt", r), "cb"], [PB(bt)])
            evac(mixT[:, 4:6, sl8], pb[bt][:, 0:256].rearrange("p (c t) -> p c t", c=2), [PB(bt)],
                 [("mix", 4, t8 // 4), ("mix", 5, t8 // 4)])

        rb_s1(RU[0])
        interleave([rb_s2(RU[0])])
        rb_s1(RU[1])
        for i in range(8):
            rb_s3(RU[i])
            if i >= 1:
                rb_s5(RU[i - 1])
            gl = [rb_s4(RU[i]), c_step(i)]
            if i + 1 < 8:
                gl.insert(0, rb_s2(RU[i + 1]))
            interleave(gl)
            if i + 2 < 8:
                rb_s1(RU[i + 2])
        rb_s5(RU[7])

        if kind == 0:
            mod_blocks(l, [7, 8, 9])
        ck(6)
        K.ev_mode = "dve"
        sCq = ring_next()
        wload(sCq, [(0, 1792, 384)], win)
        if lat:
            sCqs = ring_next()
            swap_copy(sCqs, sCq, 384, 8)
        for h in range(4):
            if not lat:
                for tti, b in proj_fm(sCq, h * 96, 96):
                    evac(qT[0:96, h, tti * 512:(tti + 1) * 512], pb[b][0:96, :], [PB(b)], [("qT", h, tti)])
            else:
                ga = proj_fm(sCq, h * 96, 96)
                gb = proj_fm(sCqs, h * 96, 96)
                for (tti, bA), (_, bB) in zip(ga, gb):
                    sl = slice(tti * 512, (tti + 1) * 512)
                    act(qT[0:64, h, sl], pb[bA][0:64, :], AF.Copy, [PB(bA)], [("qT", h, tti)])
                    rope_evac(qT[64:96, h, sl], bA, bB, slice(64, 96), tti, 2, 3, [("qT", h, tti)])
        nkt = 10 if lat else 8
        qtiles = [(0, 512), (512, 512)] if lat else [(s_ * 256, 256) for s_ in range(4)]
        built = set()

        def mla_build(h):
            if h in built:
                return
            built.add(h)
            r2 = h % 2
            for c0 in range(0, nkt * 128, 512):
                n = min(512, nkt * 128 - c0)
                b = nbank()
                mm(pb[b][0:64, 0:n], wkvb[:, l, h * 128:h * 128 + 64], ckvnT[:, c0:c0 + n], True, True,
                   ["wkvb"] + [("ckvnT", c0 // 128 + i) for i in range(n // 128)], [PB(b)])
                evac(kch[0:64, r2, c0:c0 + n], pb[b][0:64, 0:n], [PB(b)], [("kchn", r2)])
            for k0 in range(0, nkt, 8):
                nk_ = min(8, nkt - k0)
                b = nbank()
                for kk in range(nk_):
                    kt = k0 + kk
                    mm(pb[b][:, kk * 64:(kk + 1) * 64], ckvnT[:, kt * 128:(kt + 1) * 128],
                       wkvb[:, l, h * 128 + 64:h * 128 + 128], True, True, ["wkvb", ("ckvnT", kt)], [PB(b)])
                evac(vch[:, r2, k0:k0 + nk_, 0:64], pb[b][:, 0:nk_ * 64].rearrange("p (k e) -> p k e", e=64), [PB(b)],
                     [("vch", r2)])

        units = []
        for h in range(4):
            for qi, (q0, qn) in enumerate(qtiles):
                kts = list(range(10)) if lat else [2 * qi, 2 * qi + 1]
                for ii, kt in enumerate(kts):
                    units.append(dict(h=h, q0=q0, qn=qn, kt=kt, first=ii == 0, last=ii == len(kts) - 1,
                                      po=5 + (h * len(qtiles) + qi) % 3))

        def mc_s1(u):
            h, q0, qn, kt = u["h"], u["q0"], u["qn"], u["kt"]
            mla_build(h)
            r2 = h % 2
            b = nbank(hold=True)
            u["b"] = b
            mm(pb[b][:, 0:qn], kch[0:96, r2, kt * 128:(kt + 1) * 128], qT[0:96, h, q0:q0 + qn], True, True,
               [("kchn", r2), ("kchr", r2), ("qT", h, q0 // 512)], [PB(b)])

        def mc_s23(u):
            h, q0, qn, kt, b, po = u["h"], u["q0"], u["qn"], u["kt"], u["b"], u["po"]
            hs = slice((h % 2) * 64, (h % 2) * 64 + 64)
            r2 = h % 2
            r = K.uid % 3
            K.uid += 1
            act(PT[r][:, 0:qn], pb[b][:, 0:qn], AF.Exp, [PB(b)], [("PT", r)], scale=MLA_SCALE)
            release(b)
            mm(pb[po][:, 0:qn], vch[:, r2, kt, :], PT[r][:, 0:qn], u["first"], u["last"],
               [("vch", r2), ("PT", r)], [PB(po)])
            if u["last"]:
                pend.append(u)

        def mc_fin(u):
            h, q0, qn, po = u["h"], u["q0"], u["qn"], u["po"]
            hs = slice((h % 2) * 64, (h % 2) * 64 + 64)
            if True:
                act(rec[64:128, 0, 0:qn], pb[po][64:128, 0:qn], AF.Ln, [PB(po)], [("rec", 0)])
                act(rec[64:128, 0, 0:qn], rec[64:128, 0, 0:qn], AF.Exp, [("rec", 0)], [("rec", 0)], scale=-1.0)
                tt(mixT[hs, 6 + h // 2, q0:q0 + qn], pb[po][0:64, 0:qn], rec[64:128, 0, 0:qn], ALU.mult,
                   [PB(po), ("rec", 0)], [("mix", 6 + h // 2, q0 // 512)])

        DEPTH_C = 3
        for i in range(min(DEPTH_C, len(units))):
            mc_s1(units[i])
        pend = []
        for i in range(len(units)):
            mc_s23(units[i])
            if i + DEPTH_C < len(units):
                mc_s1(units[i + DEPTH_C])
            if units[i]["last"] and len(pend) > 1:
                mc_fin(pend.pop(0))
        while pend:
            mc_fin(pend.pop(0))

        ck(7)
        K.ev_mode = "mix"
        wo = wout_d[l]
        slots = []
        for half in range(2):
            s = ring_next()
            slots.append(s)
            cs = slice(half * 512, (half + 1) * 512)
            for ca in range(4):
                dma("pool", ring[s][0:64, ca, :], wo[64 * ca:64 * ca + 64, cs], (), [("ring", s)])
                dma("pool", ring[s][64:128, ca, :], wo[256 + 64 * ca:256 + 64 * ca + 64, cs], (), [("ring", s)])
            dma("pool", ring[s][:, 4:8, :], wo[512:1024, cs].rearrange("(k p) n -> p k n", p=128), (), [("ring", s)])
        for tti in range(2):
            sl = slice(tti * 512, (tti + 1) * 512)
            for m in range(8):
                b = nbank()
                s = slots[m // 4]
                for k in range(8):
                    mm(pb[b][:, :], ring[s][:, k, (m % 4) * 128:(m % 4 + 1) * 128], mixT[:, k, sl], k == 0, k == 7,
                       [("ring", s), ("mix", k, tti)], [PB(b)])
                stt(xT[:, m, sl], pb[b][:, :], gate1(m), xT[:, m, sl], ALU.mult, ALU.add,
                    [PB(b), ("modT", l), ("x", m, tti)], [("x", m, tti)])

        if kind == 0:
            mod_blocks(l, [10, 11])
        ck(8)
        norm_mod(lambda k: g2[:, l, kind, k:k + 1], sh2,
                 lambda k, tti: (hT[:, k, tti * 512:(tti + 1) * 512], [("h", k, tti)]))
        def load_mlp(fb):
            su = ring_next()
            wload(su, [(0, fb * 512, 512)], wup_d[l])
            sd = ring_next()
            for k4 in range(4):
                dma("pool", ring[sd][:, 2 * k4:2 * k4 + 2, :],
                    wdn_d[l, fb * 512 + k4 * 128:fb * 512 + (k4 + 1) * 128, :].rearrange("p (h n) -> p h n", h=2),
                    (), [("ring", sd)])
            return su, sd

        loaded = {}

        def mlp_up(fb, tti):
            if fb not in loaded:
                loaded[fb] = load_mlp(fb)
            su, sd = loaded[fb]
            sl = slice(tti * 512, (tti + 1) * 512)
            ab = (fb * 2 + tti) % 2
            for mi in range(4):
                b = nbank(0, 4)
                for k in range(8):
                    mm(pb[b][:, :], ring[su][:, k, mi * 128:(mi + 1) * 128], hT[:, k, sl], k == 0, k == 7,
                       [("ring", su), ("h", k, tti)], [PB(b)])
                r = mi % 2
                act(tmpg[:, r, :], pb[b][:, :], AF.Relu, [PB(b)], [("tmpg", r)])
                act(qT[:, mi, ab * 512:(ab + 1) * 512], tmpg[:, r, :], AF.Square, [("tmpg", r)], [("qT", mi, ab)])

        def mlp_down(fb, tti):
            su, sd = loaded[fb]
            sl = slice(tti * 512, (tti + 1) * 512)
            ab = (fb * 2 + tti) % 2
            for m in range(8):
                b = nbank(4, 8)
                for k4 in range(4):
                    mm(pb[b][:, :], ring[sd][:, 2 * k4 + m // 4, (m % 4) * 128:(m % 4 + 1) * 128],
                       qT[:, k4, ab * 512:(ab + 1) * 512], k4 == 0, k4 == 3, [("ring", sd), ("qT", k4, ab)], [PB(b)])
                stt(xT[:, m, sl], pb[b][:, :], gate2(m), xT[:, m, sl], ALU.mult, ALU.add,
                    [PB(b), ("modT", l), ("x", m, tti)], [("x", m, tti)])

        mu = [(fb, tti) for fb in range(8) for tti in range(2)]
        mlp_up(*mu[0])
        for i in range(len(mu)):
            if i + 1 < len(mu):
                mlp_up(*mu[i + 1])
            mlp_down(*mu[i])
            if kind == 0 and l == 0 and i in (2, 6, 10, 13) and not _DEV.get('nopref'):
                mod_blocks(1, [(2, 6, 10, 13).index(i)])

    try:
        load_x(0)
        setup_mod()
        ck(1)
        if not _DEV.get('skip0'):
            run_pass(0)
        ck(20)
        run_pass(1)
    except _Stop:
        pass

    names = S.sem_names()
    sems = {n: K.es.enter_context(nc.semaphore(n)) for n in names}
    block = K.es.enter_context(nc.Block())
    S.emit(block, sems)
    K.es.close()
    return nc


_CACHE = {}
_DEV = {}


class _Stop(Exception):
    pass


_MARKS = []


def ck(n):
    if _DEV.get('sched') is not None:
        _MARKS.append((n, _DEV['sched'].cnt['pe']))
    if _DEV.get('stop') == n:
        raise _Stop()


def kernel(x_prompt, x_sample, cache_win_k, cache_win_v, cache_mla_ckv, cache_mla_krope, state_ret,
           c, c_ctx, w_mod, b_mod, norm1, norm2, w_in, win_sink, ret_decay, ret_gn, mla_kv_norm,
           w_kv_b, w_out, w_up, w_down, final_norm):
    f = lambda a: np.ascontiguousarray(np.asarray(a, dtype=np.float32))
    x_prompt, x_sample = f(x_prompt), f(x_sample)
    cfa, cb, rope = host_consts()
    colT = lambda v: f(v).reshape(-1, 128).T
    prow = np.zeros((128, 2, 400), np.float32)
    for l in range(2):
        prow[:, l, 0:256] = f(ret_gn)[l][None, :]
        prow[:, l, 256:384] = f(mla_kv_norm)[l][None, :]
        prow[:, l, 384:392] = f(win_sink)[l][None, :]
        prow[:, l, 392:400] = f(ret_decay)[l].reshape(-1)[None, :]
    shared = {"cfa": cfa, "cb": cb, "rope": rope, "prow": prow,
              "w_mod": f(w_mod), "w_in": f(w_in), "w_kv_b": f(w_kv_b), "w_out": f(w_out),
              "w_up": f(w_up), "w_down": f(w_down)}
    in_maps = []
    for i in range(NCORES):
        pcol = np.concatenate([colT(f(norm1)[0]), colT(f(norm1)[1]), colT(f(norm2)[0]), colT(f(norm2)[1]),
                               colT(final_norm), colT(f(b_mod)[0]), colT(f(b_mod)[1]), colT(c_ctx),
                               colT(f(c)[i])], axis=1)
        m = dict(shared)
        m["pcol"] = f(pcol)
        m["xin"] = np.stack([x_prompt[4 * i:4 * i + 4].reshape(T, 1024), x_sample[i]], 0)
        m["ck"] = f(cache_win_k)[i].reshape(2, 256, 128)
        m["cv"] = f(cache_win_v)[i].reshape(2, 256, 128)
        m["cckv"] = f(cache_mla_ckv)[i]
        m["ckr"] = f(cache_mla_krope)[i]
        m["cst"] = f(state_ret)[i]
        in_maps.append(m)
    if _DEV.get("only_core0"):
        in_maps = in_maps[:1]
    if "nc" not in _CACHE:
        _CACHE["nc"] = build_program()
    res = run_bass_kernel_spmd(_CACHE["nc"], in_maps, core_ids=list(range(len(in_maps))))
    R = res.results
    y_prompt = np.concatenate([r["y"][0].reshape(4, 256, 1024) for r in R], 0)
    y_sample = np.stack([r["y"][1] for r in R], 0)
    nk = np.concatenate([r["nk"].reshape(4, 2, 256, 2, 64) for r in R], 0)
    nv = np.concatenate([r["nv"].reshape(4, 2, 256, 2, 64) for r in R], 0)
    nckv = np.concatenate([r["nckv"] for r in R], 0)
    nkr = np.concatenate([r["nkr"] for r in R], 0)
    nst = np.concatenate([r["nst"] for r in R], 0)
    return (y_prompt.astype(np.float32), y_sample.astype(np.float32), nk.astype(np.float32), nv.astype(np.float32),
            nckv.astype(np.float32), nkr.astype(np.float32), nst.astype(np.float32))
```
